# Optimizing a Trainium2 kernel written in Bass

```python
import jax, jax.numpy as jnp
from jax import lax
import numpy as np

D_MODEL = 1024
BATCH = 8
SEQ = 4096
DEPTH = 1
DEC_BATCH = 32
DEC_SEQ = 2048
PAST_LEN = 128

HEAD_DIM = 64
A_HEADS = D_MODEL // 128
A_KV_HEADS = 2
A_GROUP = A_HEADS // A_KV_HEADS
B_HEADS = D_MODEL // 128
D_FF = 4 * D_MODEL
GRID_W = 64
Q_BLOCK = 128
NA_WIN_ROWS = 8
NA_WIN_COLS = 16
ROPE_THETA = 10000.0
AXIS_ROPE_DIM = HEAD_DIM // 2
EPS = 1e-6
NEG_INF = -1e30

A_Q_W = A_HEADS * HEAD_DIM
A_KV_W = A_KV_HEADS * HEAD_DIM
B_W = B_HEADS * HEAD_DIM
W_IN_SPLITS = (A_Q_W, A_KV_W, A_KV_W, B_W, B_W, B_W, D_MODEL, D_MODEL)
W_IN_COLS = sum(W_IN_SPLITS)

kernel_name = "gated_gqa_axialrope_natten_encoder"


def _rmsnorm(x, g):
    xf = x.astype(jnp.float32)
    y = xf * lax.rsqrt(jnp.mean(xf * xf, axis=-1, keepdims=True) + EPS)
    return (y * g.astype(jnp.float32)).astype(x.dtype)


def _rope_tables(seq_len):
    t = jnp.arange(seq_len, dtype=jnp.int32)
    row = (t // GRID_W).astype(jnp.float32)
    col = (t % GRID_W).astype(jnp.float32)
    inv = ROPE_THETA ** (-jnp.arange(0, AXIS_ROPE_DIM, 2, dtype=jnp.float32) / AXIS_ROPE_DIM)
    ang = jnp.concatenate([row[:, None] * inv, col[:, None] * inv], axis=-1)
    return jnp.cos(ang), jnp.sin(ang)


def _apply_rope(x, cos, sin):
    shp = x.shape
    xf = x.astype(jnp.float32).reshape(shp[:-1] + (shp[-1] // 2, 2))
    bshape = (1, shp[1]) + (1,) * (x.ndim - 3) + (shp[-1] // 2,)
    c = cos.reshape(bshape)
    s = sin.reshape(bshape)
    x0, x1 = xf[..., 0], xf[..., 1]
    out = jnp.stack([x0 * c - x1 * s, x0 * s + x1 * c], axis=-1)
    return out.reshape(shp).astype(x.dtype)


def _gqa_attention(q, k, v):
    B, S = q.shape[0], q.shape[1]
    nblk = S // Q_BLOCK
    scale = HEAD_DIM ** -0.5
    qb = q.reshape(B, nblk, Q_BLOCK, A_KV_HEADS, A_GROUP, HEAD_DIM).swapaxes(0, 1)

    def block(qi):
        s = jnp.einsum('bqkgd,bskd->bkgqs', qi, k, preferred_element_type=jnp.float32) * scale
        p = jax.nn.softmax(s, axis=-1)
        return jnp.einsum('bkgqs,bskd->bqkgd', p.astype(v.dtype), v)

    o = lax.map(block, qb)
    return o.swapaxes(0, 1).reshape(B, S, A_Q_W)


def _neighbourhood_attention(q, k, v, rpb):
    B, S = q.shape[0], q.shape[1]
    rows = S // GRID_W
    wr = min(NA_WIN_ROWS, rows)
    q_rows = Q_BLOCK // GRID_W
    band = min(wr + q_rows - 1, rows)
    nkeys = band * GRID_W
    nblk = S // Q_BLOCK
    scale = HEAD_DIM ** -0.5
    kg = k.reshape(B, rows, GRID_W, B_HEADS, HEAD_DIM)
    vg = v.reshape(B, rows, GRID_W, B_HEADS, HEAD_DIM)
    qb = q.reshape(B, nblk, Q_BLOCK, B_HEADS, HEAD_DIM).swapaxes(0, 1)

    q_r_local = jnp.arange(Q_BLOCK, dtype=jnp.int32) // GRID_W
    q_c = jnp.arange(Q_BLOCK, dtype=jnp.int32) % GRID_W
    k_r_local = jnp.arange(nkeys, dtype=jnp.int32) // GRID_W
    k_c = jnp.arange(nkeys, dtype=jnp.int32) % GRID_W
    col_start = jnp.clip(q_c - NA_WIN_COLS // 2, 0, GRID_W - NA_WIN_COLS)
    col_mask = (k_c[None, :] >= col_start[:, None]) & (k_c[None, :] < col_start[:, None] + NA_WIN_COLS)
    dc_idx = jnp.clip(k_c[None, :] - q_c[:, None] + NA_WIN_COLS - 1, 0, 2 * NA_WIN_COLS - 2)

    def block(args):
        blk, qi = args
        q_r = blk * q_rows + q_r_local
        row_start = jnp.clip(q_r - wr // 2, 0, rows - wr)
        b0 = jnp.minimum(row_start[0], rows - band)
        kb = lax.dynamic_slice_in_dim(kg, b0, band, axis=1).reshape(B, nkeys, B_HEADS, HEAD_DIM)
        vb = lax.dynamic_slice_in_dim(vg, b0, band, axis=1).reshape(B, nkeys, B_HEADS, HEAD_DIM)
        k_r = b0 + k_r_local
        row_mask = (k_r[None, :] >= row_start[:, None]) & (k_r[None, :] < row_start[:, None] + wr)
        mask = row_mask & col_mask
        dr_idx = jnp.clip(k_r[None, :] - q_r[:, None] + NA_WIN_ROWS - 1, 0, 2 * NA_WIN_ROWS - 2)
        bias = rpb[:, dr_idx, dc_idx].astype(jnp.float32)
        s = jnp.einsum('bqhd,bkhd->bhqk', qi, kb, preferred_element_type=jnp.float32) * scale + bias[None]
        s = jnp.where(mask[None, None], s, NEG_INF)
        p = jax.nn.softmax(s, axis=-1)
        return jnp.einsum('bhqk,bkhd->bqhd', p.astype(vb.dtype), vb)

    o = lax.map(block, (jnp.arange(nblk, dtype=jnp.int32), qb))
    return o.swapaxes(0, 1).reshape(B, S, B_W)


def _token_mixer(xn, w_in, q_norm_g, k_norm_g, rpb, w_proj_a, w_proj_b, w_out):
    B, S, _ = xn.shape
    z = xn @ w_in
    idx = tuple(int(i) for i in np.cumsum(W_IN_SPLITS)[:-1])
    qa, ka, va, qb, kb, vb, ga, gb = jnp.split(z, idx, axis=-1)
    cos, sin = _rope_tables(S)
    qa = _apply_rope(_rmsnorm(qa.reshape(B, S, A_KV_HEADS, A_GROUP, HEAD_DIM), q_norm_g), cos, sin)
    ka = _apply_rope(_rmsnorm(ka.reshape(B, S, A_KV_HEADS, HEAD_DIM), k_norm_g), cos, sin)
    va = va.reshape(B, S, A_KV_HEADS, HEAD_DIM)
    o_a = _gqa_attention(qa, ka, va)
    o_b = _neighbourhood_attention(qb.reshape(B, S, B_HEADS, HEAD_DIM),
                                   kb.reshape(B, S, B_HEADS, HEAD_DIM),
                                   vb.reshape(B, S, B_HEADS, HEAD_DIM), rpb)
    merged = jax.nn.sigmoid(ga) * (o_a @ w_proj_a) + jax.nn.sigmoid(gb) * (o_b @ w_proj_b)
    return merged @ w_out


def _trunk(x, norm_mix_g, w_in, a_q_norm_g, a_k_norm_g, b_rel_pos_bias, w_proj_a, w_proj_b,
           w_out, norm_mlp_g, w_mlp_up, w_mlp_down, norm_final_g):
    h = x
    for l in range(DEPTH):
        h = h + _token_mixer(_rmsnorm(h, norm_mix_g[l]), w_in[l], a_q_norm_g[l], a_k_norm_g[l],
                             b_rel_pos_bias[l], w_proj_a[l], w_proj_b[l], w_out[l])
        hn = _rmsnorm(h, norm_mlp_g[l])
        h = h + jnp.square(jax.nn.relu(hn @ w_mlp_up[l])) @ w_mlp_down[l]
    return _rmsnorm(h, norm_final_g)


def setup_inputs(seed: int = 0) -> dict:
    key = jax.random.key(seed)
    ks = jax.random.split(key, 16)
    f32 = jnp.float32
    nrm = lambda k, shape, s: jax.random.normal(k, shape, f32) * s
    return {
        "x_prompt": nrm(ks[0], (BATCH, SEQ, D_MODEL), 1.0),
        "x_sample": nrm(ks[1], (DEC_BATCH, DEC_SEQ, D_MODEL), 1.0),
        "norm_mix_g": 1.0 + nrm(ks[2], (DEPTH, D_MODEL), 0.02),
        "w_in": nrm(ks[3], (DEPTH, D_MODEL, W_IN_COLS), D_MODEL ** -0.5),
        "a_q_norm_g": 1.0 + nrm(ks[4], (DEPTH, HEAD_DIM), 0.02),
        "a_k_norm_g": 1.0 + nrm(ks[5], (DEPTH, HEAD_DIM), 0.02),
        "b_rel_pos_bias": nrm(ks[6], (DEPTH, B_HEADS, 2 * NA_WIN_ROWS - 1, 2 * NA_WIN_COLS - 1), 0.1),
        "w_proj_a": nrm(ks[7], (DEPTH, A_Q_W, D_MODEL), A_Q_W ** -0.5),
        "w_proj_b": nrm(ks[8], (DEPTH, B_W, D_MODEL), B_W ** -0.5),
        "w_out": nrm(ks[9], (DEPTH, D_MODEL, D_MODEL), D_MODEL ** -0.5),
        "norm_mlp_g": 1.0 + nrm(ks[10], (DEPTH, D_MODEL), 0.02),
        "w_mlp_up": nrm(ks[11], (DEPTH, D_MODEL, D_FF), D_MODEL ** -0.5),
        "w_mlp_down": nrm(ks[12], (DEPTH, D_FF, D_MODEL), D_FF ** -0.5),
        "norm_final_g": 1.0 + nrm(ks[13], (D_MODEL,), 0.02),
    }


def reference(x_prompt, x_sample, norm_mix_g, w_in, a_q_norm_g, a_k_norm_g, b_rel_pos_bias,
              w_proj_a, w_proj_b, w_out, norm_mlp_g, w_mlp_up, w_mlp_down, norm_final_g):
    y_prompt = _trunk(x_prompt, norm_mix_g, w_in, a_q_norm_g, a_k_norm_g, b_rel_pos_bias,
                      w_proj_a, w_proj_b, w_out, norm_mlp_g, w_mlp_up, w_mlp_down, norm_final_g)
    y_sample = _trunk(x_sample, norm_mix_g, w_in, a_q_norm_g, a_k_norm_g, b_rel_pos_bias,
                      w_proj_a, w_proj_b, w_out, norm_mlp_g, w_mlp_up, w_mlp_down, norm_final_g)
    return (y_prompt, y_sample)
```

```python
import numpy as np
import ml_dtypes
from contextlib import ExitStack
import concourse.bass as bass
import concourse.mybir as mybir
from concourse.bass_utils import run_bass_kernel_spmd

F32 = mybir.dt.float32
BF16 = mybir.dt.bfloat16
ALU = mybir.AluOpType
AF = mybir.ActivationFunctionType

D = 1024
T = 512
EPS = 1e-6
QSCALE = 0.125
GRID_W = 64


class Res:
    __slots__ = ("name", "lw", "rd")

    def __init__(self, name):
        self.name = name
        self.lw = None
        self.rd = {}


class Lane:
    def __init__(self, name, is_dma=False):
        self.name = name
        self.is_dma = is_dma
        self.sem = None
        self.ops = []
        self.count = 0
        self.seen = {}


class Op:
    __slots__ = ("fn", "waits", "signal", "sigval", "chan")

    def __init__(self, fn):
        self.fn = fn
        self.waits = []
        self.signal = False
        self.sigval = 0
        self.chan = None


class _Noop:
    def then_inc(self, *a, **k):
        return self


class Sched:
    def __init__(self, nc):
        self.nc = nc
        self.pe = Lane("pe")
        self.act = Lane("act")
        self.dve = Lane("dve")
        self.pool = Lane("pool")
        self.sp = Lane("sp")
        self.engines = [self.pe, self.act, self.dve, self.pool, self.sp]
        self.chans = []

    def chan(self, name):
        c = Lane(name, is_dma=True)
        self.chans.append(c)
        return c

    def _deps(self, lane, reads, writes):
        deps = {}

        def add(l, i, same_ok):
            if l is lane and not same_ok:
                return
            if deps.get(l, -1) < i:
                deps[l] = i
        for r in reads:
            if r.lw is not None:
                add(r.lw[0], r.lw[1], True)
        for w in writes:
            if w.lw is not None:
                add(w.lw[0], w.lw[1], False)
            for l, i in w.rd.items():
                add(l, i, False)
        out = []
        for l, i in deps.items():
            if l.is_dma:
                i = l.count - 1
            if lane.seen.get(l, -1) >= i:
                continue
            lane.seen[l] = i
            out.append((l, i))
        return out

    def op(self, lane, fn, reads=(), writes=()):
        o = Op(fn)
        o.waits = self._deps(lane, reads, writes)
        idx = len(lane.ops)
        lane.ops.append(o)
        for r in reads:
            if r.rd.get(lane, -1) < idx:
                r.rd[lane] = idx
        for w in writes:
            w.lw = (lane, idx)
            w.rd = {}
        return o

    def dma(self, queue, chan, fn, reads=(), writes=()):
        o = Op(fn)
        o.chan = chan
        o.waits = self._deps(queue, reads, writes)
        queue.ops.append(o)
        idx = chan.count
        chan.count += 1
        for r in reads:
            if r.rd.get(chan, -1) < idx:
                r.rd[chan] = idx
        for w in writes:
            w.lw = (chan, idx)
            w.rd = {}
        return o

    def final_wait(self, lane, chans):
        o = Op(lambda e: _Noop())
        for c in chans:
            if c.count > 0:
                o.waits.append((c, c.count - 1))
        lane.ops.append(o)

    def finalize_and_emit(self, stack):
        nc = self.nc
        for l in self.engines + self.chans:
            l.sem = stack.enter_context(nc.semaphore("s_" + l.name))
        for l in self.engines:
            for o in l.ops:
                for (dl, di) in o.waits:
                    if not dl.is_dma:
                        dl.ops[di].signal = True
        for l in self.engines:
            v = 0
            for o in l.ops:
                if o.chan is None and o.signal:
                    v += 1
                o.sigval = v
        block = stack.enter_context(nc.Block())

        def emit(lane):
            def body(e):
                for o in lane.ops:
                    for (dl, di) in o.waits:
                        if dl.is_dma:
                            e.wait_ge(dl.sem, 16 * (di + 1))
                        else:
                            e.wait_ge(dl.sem, dl.ops[di].sigval)
                    ins = o.fn(e)
                    if o.chan is not None:
                        ins.then_inc(o.chan.sem, 16)
                    elif o.signal:
                        ins.then_inc(lane.sem, 1)
            return body
        block.tensor(emit(self.pe))
        block.scalar(emit(self.act))
        block.vector(emit(self.dve))
        block.gpsimd(emit(self.pool))
        block.sync(emit(self.sp))


class Pool_:
    def __init__(self, name, ids):
        self.name = name
        self.ids = list(ids)
        self.live = {i: False for i in self.ids}
        self.ptr = 0

    def get(self):
        n = len(self.ids)
        for k in range(n):
            i = self.ids[(self.ptr + k) % n]
            if not self.live[i]:
                self.live[i] = True
                self.ptr = (self.ptr + k + 1) % n
                return i
        raise RuntimeError("pool %s exhausted" % self.name)

    def get2(self):
        n = len(self.ids) // 2
        start = (self.ptr // 2)
        for k in range(n):
            j = (start + k) % n
            a, b = self.ids[2 * j], self.ids[2 * j + 1]
            if not self.live[a] and not self.live[b]:
                self.live[a] = self.live[b] = True
                self.ptr = (2 * j + 2) % len(self.ids)
                return a
        raise RuntimeError("pool %s exhausted (pair)" % self.name)

    def free(self, i, n=1):
        for k in range(n):
            assert self.live[i + k], (self.name, i + k)
            self.live[i + k] = False


HT = {}
_idx = 0
for _name, _n in [("qA", 2), ("kvA", 1), ("qB", 2), ("kB", 2), ("vB", 2), ("gA", 4), ("gB", 4),
                  ("WpA", 2), ("WpB", 2), ("Wout", 4), ("Wup", 16), ("Wdown", 16)]:
    HT[_name] = list(range(_idx, _idx + _n))
    _idx += _n
NH = _idx

SCALE_SPEC = {}
for _n in ["qA", "kvA", "qB", "kB", "gA", "gB"]:
    for _t in HT[_n]:
        SCALE_SPEC[_t] = (8, 0)
for _q, _t in enumerate(HT["vB"]):
    SCALE_SPEC[_t] = (4, 4 * _q)
for _t in HT["Wup"]:
    SCALE_SPEC[_t] = (8, 8)

RS = {0: (0, 0), 1: (0, 0), 2: (0, 1), 3: (2, 2), 4: (2, 2)}


def block_type(b, nblk):
    if b == 0:
        return 0, 0
    if b == 1:
        return 1, 0
    if b == nblk - 2:
        return 3, nblk - 5
    if b == nblk - 1:
        return 4, nblk - 5
    return 2, b - 2


def build_program(nc, seq_lens):
    NROWS = sum(seq_lens)
    S = Sched(nc)
    st = ExitStack()

    def din(name, shape, dt):
        return nc.dram_tensor(name, shape, dt, kind="ExternalInput").ap()

    x_all = din("x_all", [NROWS, D], F32)
    wt = din("wt", [NH, 128, 2048], F32)
    gtabs_d = din("gtabs", [128, 16], F32)
    smalls_d = din("smalls", [128, 2], F32)
    gf_d = din("gfb", [128, D], F32)
    rpbT_d = din("rpbT", [32, 120], F32)
    ind_d = din("ind", [32, 4096], F32)
    cbf_d = din("cbf", [128, 384], BF16)
    tabs_d = din("tabs", [128, 256], F32)
    y_all = nc.dram_tensor("y_all", [NROWS, D], F32, kind="ExternalOutput").ap()
    wsc = nc.dram_tensor("wsc", [NH, 128, 2048], BF16).ap()
    tz_d = nc.dram_tensor("tz_d", [120, 4096], F32).ap()
    e_d = nc.dram_tensor("e_d", [5, 128, 5120], BF16).ap()

    def sb(name, shape, dt):
        return st.enter_context(nc.sbuf_tensor(name, shape, dt))

    SMAX = max(seq_lens)
    NCH_MAX = SMAX // 128
    KA = sb("KA", [128, SMAX], BF16)
    VA = sb("VA", [128, NCH_MAX, 192], BF16)
    KBr = sb("KBr", [128, 4, 3, 512], BF16)
    VBr = sb("VBr", [128, 12, 4, 192], BF16)
    tabs = sb("tabss", [128, 4, 64], F32)
    Ct = sb("Ct", [128, 512], F32)
    St = sb("St", [128, 512], F32)
    Eint = sb("Eint", [128, 5120], BF16)
    Eedge = sb("Eedge", [128, 5120], BF16)
    NSLOT = 5
    wslot = [sb("wslot%d" % k, [128, 2048], BF16) for k in range(NSLOT)]
    wkvb = sb("wkvb", [128, 2048], BF16)
    hbuf = sb("hbuf", [128, 4, D], F32)
    xin = [sb("xin%d" % k, [128, D], F32) for k in range(2)]
    xnT = [sb("xnT%d" % k, [128, 8, 512], BF16) for k in range(2)]
    arena = sb("arena", [128, 32 * 512], BF16)
    pTb = [sb("pTb%d" % k, [128, 640], BF16) for k in range(3)]
    gfb = sb("gfbs", [128, D], F32)
    NTMP = 12
    tmpf = [sb("tmp%d" % k, [128, 512], F32) for k in range(NTMP)]
    cbf = sb("cbfs", [128, 384], BF16)
    gtabs = sb("gtabss", [128, 16], F32)
    smalls = sb("smallss", [128, 2], F32)
    stats = sb("stats", [128, 16 * 12], F32)
    rpbT = sb("rpbTs", [32, 120], F32)
    q0buf = sb("q0buf", [128, 512], BF16)
    ps = st.enter_context(nc.psum_tensor("ps", [128, 4096], F32))

    ident = cbf[:, 0:128]
    bones = cbf[:, 128:256]
    rotm = cbf[:, 256:384]

    RR = {}

    def R(name):
        r = RR.get(name)
        if r is None:
            r = RR[name] = Res(name)
        return r
    Rb = [R("bank%d" % b) for b in range(8)]
    Rw = [R("wslot%d" % k) for k in range(NSLOT)]
    Rt = [R("tmp%d" % k) for k in range(NTMP)]
    Rar = [R("ar%d" % k) for k in range(32)]
    Rh = [R("h%d" % s) for s in range(4)]
    Rxin = [R("xin%d" % k) for k in range(2)]
    RxnT = [R("xnT%d" % k) for k in range(2)]
    RKB = [R("KB%d" % k) for k in range(3)]
    RVB = [R("VB%d" % k) for k in range(3)]
    RpTb = [R("pTb%d" % k) for k in range(3)]
    Rst = [R("stat%d" % k) for k in range(16)]

    ACC = Pool_("acc", [0, 1, 2, 3])
    ROT = Pool_("rot", [4, 5, 6, 7])
    ACCA = Pool_("acca", [0, 1])
    FIL = Pool_("fil", [2, 3])
    TMP = Pool_("tmp", range(NTMP))
    WS = Pool_("ws", range(NSLOT))
    PTB = Pool_("ptb", range(3))
    PTA = Pool_("pta", range(3))
    STP = Pool_("stat", range(16))

    def bank(b, n=1):
        return ps[:, b * 512:(b + n) * 512]

    def tb(k):
        return tmpf[k][:].bitcast(BF16)

    def arch(k, n=1):
        return arena[:, k * 512:(k + n) * 512]

    c_set = S.chan("cset")
    c_wsc = [S.chan("cwsc%d" % k) for k in range(NSLOT)]
    c_tz = S.chan("ctz")
    c_tzk = S.chan("ctzk")
    c_ed = S.chan("ced")
    c_w = [S.chan("cw%d" % k) for k in range(NSLOT)]
    c_xin = [S.chan("cxin%d" % k) for k in range(2)]
    c_h = S.chan("chl")
    c_out = [S.chan("cout%d" % s) for s in range(4)]
    c_edge = S.chan("cedge")

    def mm(out, lhsT, rhs, start, stop, r, w):
        S.op(S.pe, lambda e: e.matmul(out, lhsT=lhsT, rhs=rhs, start=start, stop=stop), r, w)

    def actf(out, in_, func, r, w, scale=1.0, bias=0.0, accum=None):
        if accum is None:
            S.op(S.act, lambda e: e.activation(out=out, in_=in_, func=func, scale=scale, bias=bias), r, w)
        else:
            S.op(S.act, lambda e: e.activation(out=out, in_=in_, func=func, scale=scale, bias=bias,
                                               accum_out=accum), r, w)

    def copy_on(lane, out, in_, r, w):
        if lane is S.act:
            S.op(S.act, lambda e: e.activation(out=out, in_=in_, func=AF.Copy), r, w)
        else:
            S.op(lane, lambda e: e.tensor_copy(out=out, in_=in_), r, w)

    def tt(lane, out, in0, in1, op, r, w):
        S.op(lane, lambda e: e.tensor_tensor(out=out, in0=in0, in1=in1, op=op), r, w)

    def tsmul(lane, out, in0, scalar, r, w):
        S.op(lane, lambda e: e.tensor_scalar_mul(out=out, in0=in0, scalar1=scalar), r, w)

    def tsadd(lane, out, in0, scalar, r, w):
        S.op(lane, lambda e: e.tensor_scalar_add(out=out, in0=in0, scalar1=scalar), r, w)

    def stt(lane, out, in0, scalar, in1, op0, op1, r, w):
        S.op(lane, lambda e: e.scalar_tensor_tensor(out=out, in0=in0, scalar=scalar, in1=in1,
                                                    op0=op0, op1=op1), r, w)

    def recip_ln(out, in_, r, w, bias=0.0):
        actf(out, in_, AF.Ln, r, w, bias=bias)

    def recip_exp(buf, rw):
        actf(buf, buf, AF.Exp, rw, rw, scale=-1.0)

    def memset(lane, ap, val, w):
        S.op(lane, lambda e: e.memset(ap, val), (), w)

    def sdma(chan, out, in_, r, w):
        S.dma(S.sp, chan, lambda e: e.dma_start(out=out, in_=in_), r, w)

    def pdma(chan, out, in_, r, w):
        S.dma(S.pool, chan, lambda e: e.dma_start(out=out, in_=in_), r, w)

    rr_state = {"evac": 0}

    def evac_lane():
        rr_state["evac"] += 1
        return S.act if rr_state["evac"] % 3 == 0 else S.dve

    sdma(c_set, cbf[:], cbf_d, (), [R("cbf")])
    sdma(c_set, gtabs[:], gtabs_d, (), [R("gtabs")])
    sdma(c_set, smalls[:], smalls_d, (), [R("smalls")])
    sdma(c_set, gfb[:], gf_d, (), [R("gfb")])
    sdma(c_set, tabs[:].rearrange("p a b -> p (a b)"), tabs_d, (), [R("tabs")])
    sdma(c_set, rpbT[:], rpbT_d, (), [R("rpbT")])
    hst = hbuf[:].rearrange("p s d -> p (s d)")
    arena_f = arena[:].bitcast(F32)
    stage_f = [(hst[:, 0:2048], [Rh[0], Rh[1]]), (hst[:, 2048:4096], [Rh[2], Rh[3]])]
    for q in range(4):
        stage_f.append((arena_f[:, q * 2048:(q + 1) * 2048], Rar[8 * q:8 * q + 8]))
    c_wst = [S.chan("cwst%d" % k) for k in range(len(stage_f))]
    cast_state = {"n": 0}

    def cast_load(t):
        fs = cast_state["n"] % len(stage_f)
        cast_state["n"] += 1
        src, Rsrc = stage_f[fs]
        sdma(c_wst[fs], src, wt[t], (), Rsrc)
        return fs

    def cast_one(t, fs=None):
        if fs is None:
            fs = cast_load(t)
        src, Rsrc = stage_f[fs]
        is_kv = (t == HT["kvA"][0])
        if is_kv:
            dst, Rdst, k = wkvb[:], [R("wkvb")], None
        else:
            k = WS.get()
            dst, Rdst = wslot[k][:], [Rw[k]]
        spec = SCALE_SPEC.get(t)
        lane = S.act if t % 2 == 0 else S.dve
        if spec is None:
            copy_on(lane, dst, src, Rsrc, Rdst)
        else:
            nkc, c0 = spec
            wdt = 2048 // nkc
            for kc in range(nkc):
                o_ = dst[:, kc * wdt:(kc + 1) * wdt]
                i_ = src[:, kc * wdt:(kc + 1) * wdt]
                sc = gtabs[:, c0 + kc:c0 + kc + 1]
                if lane is S.act:
                    actf(o_, i_, AF.Copy, Rsrc + [R("gtabs")], Rdst, scale=sc)
                else:
                    tsmul(S.dve, o_, i_, sc, Rsrc + [R("gtabs")], Rdst)
        if not is_kv:
            pdma(c_wsc[k], wsc[t], dst, Rdst, [R("wsc%d" % t)])
            WS.free(k)

    def cast_gen():
        ts = [t for t in range(NH) if t != HT["kvA"][0]]
        PF = len(stage_f) - 1
        pend = []
        for q, t in enumerate(ts):
            while len(pend) < PF and len(pend) + q < len(ts):
                t2 = ts[q + len(pend)]
                pend.append((t2, cast_load(t2)))
            t1, fs = pend.pop(0)
            assert t1 == t
            cast_one(t, fs)
            yield

    cast_one(HT["kvA"][0])

    ind_sb = arena_f[0:32, 0:4096]
    tz_sb = arena_f[0:120, 4096:8192]
    sdma(c_set, ind_sb, ind_d, (), Rar)
    for n in range(8):
        mm(bank(n)[0:120, :], rpbT[:, :], ind_sb[:, n * 512:(n + 1) * 512], True, True,
           Rar + [R("rpbT")], [Rb[n]])
        copy_on(S.dve if n % 2 else S.act, tz_sb[:, n * 512:(n + 1) * 512], bank(n)[0:120, :], [Rb[n]], Rar)
    sdma(c_tz, tz_d, tz_sb, Rar, [R("tz_d")])
    VBflat = VBr[:].rearrange("p a b c -> p (a b c)")
    ez = [VBflat[:, 0:3840], VBflat[:, 4608:8448]]
    for grp in range(2):
        for kr2 in range(2):
            sdma(c_tzk,
                 hst[kr2 * 64:(kr2 + 1) * 64, 0:3840].rearrange("p (a q) -> p a q", q=64),
                 tz_d[grp * 60:(grp + 1) * 60, :].rearrange("a (k q) -> k a q", q=64),
                 [R("tz_d")], Rh)
        actf(ez[grp], hst[:, 0:3840], AF.Copy, Rh, RVB, scale=1.0 / QSCALE)

    def build_E(ty, dst, Rd, lane):
        dv = dst.rearrange("p (h c q) -> p h c q", h=8, c=5)
        for ch in range(5):
            for kr2 in range(2):
                for qr2 in range(2):
                    krow = 2 * ch + kr2
                    rs = RS[ty][qr2]
                    if not (rs <= krow < rs + 8):
                        continue
                    a_ = krow - (2 * ty + qr2) + 7
                    assert 0 <= a_ <= 14
                    for grp in range(2):
                        i_ = ez[grp][kr2 * 64:(kr2 + 1) * 64, :].rearrange(
                            "p (h a q) -> p h a q", h=4, a=15)[:, :, a_, :]
                        o_ = dv[kr2 * 64:(kr2 + 1) * 64, grp * 4:(grp + 1) * 4, ch, qr2 * 64:(qr2 + 1) * 64]
                        copy_on(lane, o_, i_, RVB, Rd)

    VAflat = VA[:].rearrange("p a b -> p (a b)")
    KBflat = KBr[:].rearrange("p a b c -> p (a b c)")
    stagings = {0: (VAflat[:, 0:5120], [R("VA")], S.dve), 1: (KBflat[:, 0:5120], RKB, S.act),
                3: (Eedge[:], [R("Eedge")], S.act), 4: (arena[:, 0:5120], Rar[0:10], S.pool)}
    stagings[2] = (Eint[:], [R("Eint")], S.dve)
    for n_, ty in enumerate((2, 0, 1, 3, 4)):
        dst, Rd, lane = stagings[ty]
        memset(S.pool if n_ % 2 else S.dve, dst, -240000.0, Rd)
    for ty in (2, 1, 4, 0, 3):
        dst, Rd, lane = stagings[ty]
        build_E(ty, dst, Rd, lane)
        if ty != 2:
            sdma(c_ed, e_d[ty], dst, Rd, [R("e_d")])
    memset(S.pool, VA[:, :, 64:128], 1.0, [R("VA")])
    memset(S.pool, VBr[:, :, :, 64:128], 1.0, RVB)

    def wload(t):
        k = WS.get()
        sdma(c_w[k], wslot[k][:], wsc[t], [R("wsc%d" % t)], [Rw[k]])
        return k

    def rstd_batch(srcs):
        si = STP.get()
        n = len(srcs)
        b0 = 12 * si
        j = TMP.get()
        for q, (src_ap, Rsrc) in enumerate(srcs):
            actf(tb(j)[:, 0:1024], src_ap, AF.Square, Rsrc, [Rt[j], Rst[si]], accum=stats[:, b0 + q:b0 + q + 1])
        TMP.free(j)
        actf(stats[:, b0 + 4:b0 + 4 + n], stats[:, b0:b0 + n], AF.Ln, [Rst[si]], [Rst[si]], scale=1.0 / D, bias=EPS)
        actf(stats[:, b0 + 8:b0 + 8 + n], stats[:, b0 + 4:b0 + 4 + n], AF.Exp, [Rst[si]], [Rst[si]], scale=-0.5)
        return si, [stats[:, b0 + 8 + q:b0 + 9 + q] for q in range(n)]

    def rstd_of(src_ap, Rsrc):
        si, cols = rstd_batch([(src_ap, Rsrc)])
        return si, cols[0]

    def transpose_to(src_bf, Rsrc, dst_buf, s):
        b = ROT.get2()
        for kc in range(8):
            mm(bank(b, 2)[:, kc * 128:(kc + 1) * 128], src_bf[:, kc * 128:(kc + 1) * 128], ident,
               True, True, Rsrc + [R("cbf")], [Rb[b], Rb[b + 1]])
        copy_on(evac_lane(), xnT[dst_buf][:, :, s * 128:(s + 1) * 128],
                bank(b, 2).rearrange("p (k t) -> p k t", k=8), [Rb[b], Rb[b + 1]], [RxnT[dst_buf]])
        ROT.free(b, 2)

    xin_rr = {"k": 0}

    def norm_stage(row0, on_pool=False):
        js = []
        for s in range(4):
            k = xin_rr["k"]
            xin_rr["k"] = 1 - k
            r0 = row0 + s * 128
            (pdma if on_pool else sdma)(c_xin[k], xin[k][:], x_all[r0:r0 + 128, :], (), [Rxin[k]])
            si, rs = rstd_of(xin[k][:], [Rxin[k]])
            j = TMP.get()
            tsmul(S.dve, tb(j)[:, 0:1024], xin[k][:], rs, [Rxin[k], Rst[si]], [Rt[j]])
            STP.free(si)
            js.append(j)
        return js

    def transpose_stage(js, dst_buf):
        for s, j in enumerate(js):
            transpose_to(tb(j), [Rt[j]], dst_buf, s)
            TMP.free(j)

    def make_xnT(row0, dst_buf):
        transpose_stage(norm_stage(row0), dst_buf)

    def build_rope_tiles(tile):
        row0 = tile * 8
        for (dst, Rd, a) in ((Ct, R("Ct"), 0), (St, R("St"), 1)):
            S.op(S.pool, (lambda dst, a: lambda e: e.tensor_tensor(
                out=dst[:].rearrange("p (r c) -> p r c", c=64),
                in0=tabs[:, a, row0:row0 + 8].unsqueeze(2).to_broadcast([128, 8, 64]),
                in1=tabs[:, 2 + a, :].unsqueeze(1).to_broadcast([128, 8, 64]),
                op=ALU.add))(dst, a), [R("tabs")], [Rd])

    def headnorm_rope(zb, zpool, gcol, out_ap, Rout, spool=None):
        spool = spool or ROT
        sq = TMP.get()
        actf(tb(sq)[:, 0:512], bank(zb), AF.Square, [Rb[zb]], [Rt[sq]])
        yield
        sbk = spool.get()
        mm(bank(sbk), bones, tb(sq)[:, 0:512], True, True, [Rt[sq], R("cbf")], [Rb[sbk]])
        TMP.free(sq)
        ln = TMP.get()
        actf(tmpf[ln][:], bank(sbk), AF.Ln, [Rb[sbk]], [Rt[ln]], bias=EPS)
        spool.free(sbk)
        actf(tmpf[ln][:], tmpf[ln][:], AF.Exp, [Rt[ln]], [Rt[ln]], scale=-0.5)
        kn = TMP.get()
        stt(S.dve, tb(kn)[:, 0:512], bank(zb), gcol, tmpf[ln][:], ALU.mult, ALU.mult,
            [Rb[zb], Rt[ln], R("smalls")], [Rt[kn]])
        zpool.free(zb)
        TMP.free(ln)
        yield
        rb = spool.get()
        mm(bank(rb), rotm, tb(kn)[:, 0:512], True, True, [Rt[kn], R("cbf")], [Rb[rb]])
        t1 = TMP.get()
        tt(S.pool, tmpf[t1][:], tb(kn)[:, 0:512], Ct[:], ALU.mult, [Rt[kn], R("Ct")], [Rt[t1]])
        TMP.free(kn)
        t2 = TMP.get()
        tt(S.dve, tmpf[t2][:], bank(rb), St[:], ALU.mult, [Rb[rb], R("St")], [Rt[t2]])
        spool.free(rb)
        tt(S.pool, out_ap, tmpf[t1][:], tmpf[t2][:], ALU.add, [Rt[t1], Rt[t2]], Rout)
        TMP.free(t1)
        TMP.free(t2)
        yield

    def run_jobs(jobs, lag=1):
        active = []
        jobs = list(jobs)
        while jobs or active:
            if jobs:
                active.append(jobs.pop(0))
            nxt = []
            for g in active:
                try:
                    next(g)
                    nxt.append(g)
                except StopIteration:
                    pass
            active = nxt

    def process_sequence(row_base, SL, extra_gen=None):
        nt = SL // T
        nch = SL // 128
        nblk = nch
        wkv3 = wkvb[:].rearrange("p (k c) -> p k c", k=8)

        def p1_job(tile):
            buf = tile % 2
            js = norm_stage(row_base + tile * T)
            yield
            transpose_stage(js, buf)
            yield
            zb = ACC.get()
            for kc in range(8):
                mm(bank(zb), wkv3[:, kc, 0:128], xnT[buf][:, kc, :], kc == 0, kc == 7,
                   [R("wkvb"), RxnT[buf]], [Rb[zb]])
            for s in range(4):
                vb_ = ROT.get()
                for kc in range(8):
                    mm(bank(vb_)[:, 0:128], xnT[buf][:, kc, s * 128:(s + 1) * 128], wkv3[:, kc, 128:256],
                       kc == 0, kc == 7, [R("wkvb"), RxnT[buf]], [Rb[vb_]])
                c = tile * 4 + s
                copy_on(evac_lane(), VA[:, c, :].rearrange("p (a d) -> p a d", a=3)[:, 0:3:2, :],
                        bank(vb_)[:, 0:128].rearrange("p (a d) -> p a d", a=2), [Rb[vb_]], [R("VA")])
                ROT.free(vb_)
            hn = headnorm_rope(zb, ACC, smalls[:, 1:2], KA[:, tile * T:(tile + 1) * T], [R("KA")])
            next(hn)
            yield
            next(hn)
            yield
            build_rope_tiles(tile)
            for _ in hn:
                pass

        jobs = [p1_job(t) for t in range(nt)]
        active = []
        while jobs or active:
            if jobs:
                active.append(jobs.pop(0))
            nxt_ = []
            for g in active:
                try:
                    next(g)
                    nxt_.append(g)
                except StopIteration:
                    pass
            active = nxt_
            if extra_gen is not None:
                for _ in range(6):
                    try:
                        next(extra_gen)
                    except StopIteration:
                        extra_gen = None
                        break
        if extra_gen is not None:
            for _ in extra_gen:
                pass

        def proj_kvB(tile):
            buf = tile % 2
            slot = tile % 3
            for half, t in enumerate(HT["kB"]):
                k = wload(t)
                w3 = wslot[k][:].rearrange("p (k c) -> p k c", k=8)
                for jl in range(2):
                    jj = half * 2 + jl
                    zb = ROT.get()
                    for kc in range(8):
                        mm(bank(zb), w3[:, kc, jl * 128:(jl + 1) * 128], xnT[buf][:, kc, :], kc == 0, kc == 7,
                           [Rw[k], RxnT[buf]], [Rb[zb]])
                    copy_on(evac_lane(), KBr[:, jj, slot, :], bank(zb), [Rb[zb]], [RKB[slot]])
                    ROT.free(zb)
                WS.free(k)
            bs = [ACC.get() for _ in range(4)]
            for kq, t in enumerate(HT["vB"]):
                k = wload(t)
                w3 = wslot[k][:].rearrange("p (k c) -> p k c", k=4)
                for s in range(4):
                    for kc in range(4):
                        mm(bank(bs[s]), xnT[buf][:, kq * 4 + kc, s * 128:(s + 1) * 128], w3[:, kc, :],
                           kq == 0 and kc == 0, kq == 1 and kc == 3, [Rw[k], RxnT[buf]], [Rb[bs[s]]])
                WS.free(k)
            for s in range(4):
                c = slot * 4 + s
                copy_on(evac_lane(),
                        VBr[:, c, :, :].rearrange("p j (a d) -> p j a d", a=3)[:, :, 0:3:2, :],
                        bank(bs[s]).rearrange("p (j a d) -> p j a d", j=4, a=2), [Rb[bs[s]]], [RVB[slot]])
                ACC.free(bs[s])

        make_xnT(row_base, 0)
        proj_kvB(0)
        carry = norm_stage(row_base + T) if nt > 1 else None
        carry_q = None
        for i in range(nt):
            bufi = i % 2
            nxt_js = carry
            carry = None

            wq = {}

            def qw(name, idx, nuse):
                key = (name, idx)
                if key not in wq:
                    wq[key] = [wload(HT[name][idx]), nuse]
                ent = wq[key]
                return ent[0]

            def qw_done(name, idx):
                ent = wq[(name, idx)]
                ent[1] -= 1
                if ent[1] == 0:
                    WS.free(ent[0])

            def qA_gen(j, zpool, spool, buf=None, out_ap=None, Rout=None, k=None):
                buf = bufi if buf is None else buf
                own = k is None
                if own:
                    k = qw("qA", j // 2, 2)
                w3 = wslot[k][:].rearrange("p (k c) -> p k c", k=8)
                jl = j % 2
                zb = zpool.get()
                for kc in range(8):
                    mm(bank(zb), w3[:, kc, jl * 128:(jl + 1) * 128], xnT[buf][:, kc, :], kc == 0, kc == 7,
                       [Rw[k], RxnT[buf]], [Rb[zb]])
                    if kc in (1, 3, 5):
                        yield
                if own:
                    qw_done("qA", j // 2)
                yield from headnorm_rope(zb, zpool, smalls[:, 0:1], arch(j) if out_ap is None else out_ap,
                                         [Rar[j]] if Rout is None else Rout, spool)

            def qB_gen(jj):
                k = qw("qB", jj // 2, 2)
                w3 = wslot[k][:].rearrange("p (k c) -> p k c", k=8)
                jl = jj % 2
                zb = FIL.get()
                for kc in range(8):
                    mm(bank(zb), w3[:, kc, jl * 128:(jl + 1) * 128], xnT[bufi][:, kc, :], kc == 0, kc == 7,
                       [Rw[k], RxnT[bufi]], [Rb[zb]])
                    if kc in (1, 3, 5):
                        yield
                qw_done("qB", jj // 2)
                copy_on(S.dve, arch(4 + jj), bank(zb), [Rb[zb]], [Rar[4 + jj]])
                FIL.free(zb)
                yield

            def T_gen(s_, j_, dst_buf):
                b_ = FIL.get2()
                for kc in range(8):
                    mm(bank(b_, 2)[:, kc * 128:(kc + 1) * 128], tb(j_)[:, kc * 128:(kc + 1) * 128], ident,
                       True, True, [Rt[j_], R("cbf")], [Rb[b_], Rb[b_ + 1]])
                    if kc == 3:
                        yield
                TMP.free(j_)
                copy_on(S.dve, xnT[dst_buf][:, :, s_ * 128:(s_ + 1) * 128],
                        bank(b_, 2).rearrange("p (k t) -> p k t", k=8), [Rb[b_], Rb[b_ + 1]], [RxnT[dst_buf]])
                FIL.free(b_, 2)
                yield

            def kB_gen(jj, tile):
                buf = tile % 2
                k = qw("kB", jj // 2, 2)
                w3 = wslot[k][:].rearrange("p (k c) -> p k c", k=8)
                jl = jj % 2
                zb = FIL.get()
                for kc in range(8):
                    mm(bank(zb), w3[:, kc, jl * 128:(jl + 1) * 128], xnT[buf][:, kc, :], kc == 0, kc == 7,
                       [Rw[k], RxnT[buf]], [Rb[zb]])
                    if kc in (1, 3, 5):
                        yield
                qw_done("kB", jj // 2)
                copy_on(S.dve, KBr[:, jj, tile % 3, :], bank(zb), [Rb[zb]], [RKB[tile % 3]])
                FIL.free(zb)
                yield

            def vB_gen(s_, tile):
                buf = tile % 2
                slot = tile % 3
                zb = FIL.get()
                for kq in range(2):
                    k = qw("vB", kq, 4)
                    w3 = wslot[k][:].rearrange("p (k c) -> p k c", k=4)
                    for kc in range(4):
                        mm(bank(zb), xnT[buf][:, kq * 4 + kc, s_ * 128:(s_ + 1) * 128], w3[:, kc, :],
                           kq == 0 and kc == 0, kq == 1 and kc == 3, [Rw[k], RxnT[buf]], [Rb[zb]])
                        if kc == 1:
                            yield
                    qw_done("vB", kq)
                    if kq == 0:
                        yield
                c_ = slot * 4 + s_
                copy_on(S.dve,
                        VBr[:, c_, :, :].rearrange("p j (a d) -> p j a d", a=3)[:, :, 0:3:2, :],
                        bank(zb).rearrange("p (j a d) -> p j a d", j=4, a=2), [Rb[zb]], [RVB[slot]])
                FIL.free(zb)
                yield

            def gA_gen(m):
                k = qw("gA", m // 2, 2)
                w3 = wslot[k][:].rearrange("p (k c) -> p k c", k=8)
                ml = m % 2
                zb = FIL.get()
                for kc in range(8):
                    mm(bank(zb), w3[:, kc, ml * 128:(ml + 1) * 128], xnT[bufi][:, kc, :], kc == 0, kc == 7,
                       [Rw[k], RxnT[bufi]], [Rb[zb]])
                    if kc in (1, 3, 5):
                        yield
                qw_done("gA", m // 2)
                copy_on(S.dve, arch(16 + m), bank(zb), [Rb[zb]], [Rar[16 + m]])
                FIL.free(zb)
                yield

            if i == 0:
                build_rope_tiles(0)
                for _ in qA_gen(0, ACC, ROT, out_ap=q0buf[:], Rout=[R("q0")]):
                    pass
            else:
                wq[("qA", 0)] = [carry_q, 1]
                carry_q = None
            fillers = []
            for j in (1, 2, 3):
                fillers.append((j, qA_gen(j, FIL, FIL)))
            for jj in range(4):
                fillers.append((99, qB_gen(jj)))
            if nxt_js is not None:
                for s_ in range(4):
                    fillers.append((99, T_gen(s_, nxt_js[s_], (i + 1) % 2)))
                for jj in range(4):
                    fillers.append((99, kB_gen(jj, i + 1)))
                for s_ in range(4):
                    fillers.append((99, vB_gen(s_, i + 1)))
            for m in range(8):
                fillers.append((99, gA_gen(m)))
            n_units_est = 3 * 7 + 4 * 5 + (8 + 20 + 20 if nxt_js is not None else 0) + 40
            fstate = {"done": 0}

            def pump(n=1):
                while n > 0 and fillers:
                    try:
                        next(fillers[0][1])
                        fstate["done"] += 1
                        n -= 1
                    except StopIteration:
                        fillers.pop(0)

            def drain(deadline):
                while fillers and fillers[0][0] <= deadline:
                    try:
                        next(fillers[0][1])
                        fstate["done"] += 1
                    except StopIteration:
                        fillers.pop(0)

            r0 = row_base + i * T
            pdma(c_h, hbuf[:], x_all[r0:r0 + T, :].rearrange("(s p) d -> p s d", p=128), (), Rh)

            tot_steps = 4 * nch
            for j in range(4):
                qsrc = q0buf[:] if j == 0 else arch(j)
                Rq = R("q0") if j == 0 else Rar[j]
                drain(j)
                accA = ACCA.get()
                accB = ACCA.get()
                prev = None
                for c in range(nch + 1):
                    cur = None
                    step = j * nch + c
                    while fillers and fstate["done"] < (step + 1) * n_units_est / tot_steps:
                        pump(1)
                    if c < nch:
                        r2 = ROT.get2()
                        mm(bank(r2), KA[0:64, c * 128:(c + 1) * 128], qsrc[0:64, :], True, True,
                           [R("KA"), Rq], [Rb[r2]])
                        mm(bank(r2 + 1), KA[64:128, c * 128:(c + 1) * 128], qsrc[64:128, :], True, True,
                           [R("KA"), Rq], [Rb[r2 + 1]])
                        pt = PTA.get()
                        actf(arch(24 + 2 * pt, 2), bank(r2, 2), AF.Exp, [Rb[r2], Rb[r2 + 1]],
                             [Rar[24 + 2 * pt], Rar[25 + 2 * pt]], scale=QSCALE)
                        ROT.free(r2, 2)
                        cur = (c, pt)
                    if prev is not None:
                        pc, ppt = prev
                        mm(bank(accA), VA[:, pc, 0:128], arch(24 + 2 * ppt), pc == 0, pc == nch - 1,
                           [R("VA"), Rar[24 + 2 * ppt]], [Rb[accA]])
                        mm(bank(accB), VA[:, pc, 64:192], arch(25 + 2 * ppt), pc == 0, pc == nch - 1,
                           [R("VA"), Rar[25 + 2 * ppt]], [Rb[accB]])
                        PTA.free(ppt)
                    prev = cur
                rc = TMP.get()
                rc2 = TMP.get()
                recip_ln(tmpf[rc][64:128, :], bank(accA)[64:128, :], [Rb[accA]], [Rt[rc]])
                recip_ln(tmpf[rc2][0:64, :], bank(accB)[0:64, :], [Rb[accB]], [Rt[rc2]])
                recip_exp(tmpf[rc][64:128, :], [Rt[rc]])
                recip_exp(tmpf[rc2][0:64, :], [Rt[rc2]])
                tt(S.dve, arch(8 + j)[0:64, :], bank(accA)[0:64, :], tmpf[rc][64:128, :], ALU.mult,
                   [Rb[accA], Rt[rc]], [Rar[8 + j]])
                tt(S.dve, arch(8 + j)[64:128, :], bank(accB)[64:128, :], tmpf[rc2][0:64, :], ALU.mult,
                   [Rb[accB], Rt[rc2]], [Rar[8 + j]])
                TMP.free(rc)
                TMP.free(rc2)
                ACCA.free(accA)
                ACCA.free(accB)
            drain(99)

            GBS = [0, 1, 2, 3, 24, 25, 26, 27]

            def tA_gen(m):
                k = qw("WpA", m // 4, 4)
                wp = wslot[k][:].rearrange("p (k c) -> p k c", k=4)
                ml = m % 4
                pbk = FIL.get()
                for kc in range(4):
                    mm(bank(pbk), wp[:, kc, ml * 128:(ml + 1) * 128], arch(8 + kc), kc == 0, kc == 3,
                       [Rw[k], Rar[8 + kc]], [Rb[pbk]])
                    if kc == 1:
                        yield
                qw_done("WpA", m // 4)
                eg = TMP.get()
                actf(tmpf[eg][:], arch(16 + m), AF.Exp, [Rar[16 + m]], [Rt[eg]], scale=-1.0)
                actf(tmpf[eg][:], tmpf[eg][:], AF.Ln, [Rt[eg]], [Rt[eg]], bias=1.0)
                yield
                actf(tmpf[eg][:], tmpf[eg][:], AF.Exp, [Rt[eg]], [Rt[eg]], scale=-1.0)
                tt(S.dve, arch(16 + m), bank(pbk), tmpf[eg][:], ALU.mult, [Rb[pbk], Rt[eg]], [Rar[16 + m]])
                FIL.free(pbk)
                TMP.free(eg)
                yield

            def gB_gen(m):
                k = qw("gB", m // 2, 2)
                w3 = wslot[k][:].rearrange("p (k c) -> p k c", k=8)
                ml = m % 2
                zb = FIL.get()
                for kc in range(8):
                    mm(bank(zb), w3[:, kc, ml * 128:(ml + 1) * 128], xnT[bufi][:, kc, :], kc == 0, kc == 7,
                       [Rw[k], RxnT[bufi]], [Rb[zb]])
                    if kc in (1, 3, 5):
                        yield
                qw_done("gB", m // 2)
                copy_on(S.dve, arch(GBS[m]), bank(zb), [Rb[zb]], [Rar[GBS[m]]])
                FIL.free(zb)
                yield

            for m in range(8):
                fillers.append((99, tA_gen(m)))
            for m in range(8):
                fillers.append((99, gB_gen(m)))
            n_units_b = 8 * 4 + 8 * 5
            fstate["done"] = 0

            for blk in range(4):
                b = i * 4 + blk
                ty, cs0 = block_type(b, nblk)
                if ty == 2:
                    E, RE = Eint, R("Eint")
                else:
                    sdma(c_edge, Eedge[:], e_d[ty], [R("e_d")], [R("Eedge")])
                    E, RE = Eedge, R("Eedge")
                Ev = E[:].rearrange("p (h f) -> p h f", h=8)
                bx = [ACCA.get(), ACCA.get()]
                prev = None
                for h in range(9):
                    cur = None
                    while fillers and fstate["done"] < (blk * 9 + h + 1) * n_units_b / 36.0:
                        pump(1)
                    if h < 8:
                        jj, hp = h // 2, h % 2
                        r2 = ROT.get2()
                        mm(bank(r2, 2)[:, 0:512], ident, Ev[:, h, 0:512], True, False,
                           [RE, R("cbf")], [Rb[r2], Rb[r2 + 1]])
                        mm(bank(r2, 2)[:, 512:640], ident, Ev[:, h, 512:640], True, False,
                           [RE, R("cbf")], [Rb[r2], Rb[r2 + 1]])
                        for ch in range(5):
                            cs = cs0 + ch
                            tt_, s_ = cs // 4, cs % 4
                            mm(bank(r2, 2)[:, ch * 128:(ch + 1) * 128],
                               KBr[hp * 64:(hp + 1) * 64, jj, tt_ % 3, s_ * 128:(s_ + 1) * 128],
                               arch(4 + jj)[hp * 64:(hp + 1) * 64, blk * 128:(blk + 1) * 128],
                               False, ch in (3, 4),
                               [RKB[tt_ % 3], Rar[4 + jj]], [Rb[r2], Rb[r2 + 1]])

                        pt = PTB.get()
                        actf(pTb[pt][:], bank(r2, 2)[:, 0:640], AF.Exp, [Rb[r2], Rb[r2 + 1]], [RpTb[pt]],
                             scale=QSCALE)
                        ROT.free(r2, 2)
                        cur = (h, pt)
                    if prev is not None:
                        ph, ppt = prev
                        jj, hp = ph // 2, ph % 2
                        ob = bx[ph // 4]
                        for ch in range(5):
                            cs = cs0 + ch
                            tt_, s_ = cs // 4, cs % 4
                            mm(bank(ob)[:, (ph % 4) * 128:(ph % 4 + 1) * 128],
                               VBr[:, (tt_ % 3) * 4 + s_, jj, hp * 64:hp * 64 + 128],
                               pTb[ppt][:, ch * 128:(ch + 1) * 128], ch == 0, ch == 4,
                               [RVB[tt_ % 3], RpTb[ppt]], [Rb[ob]])
                        PTB.free(ppt)
                    prev = cur
                rcs = [TMP.get(), TMP.get()]
                for half in range(2):
                    recip_ln(tmpf[rcs[half]][:], bank(bx[half]), [Rb[bx[half]]], [Rt[rcs[half]]])
                for half in range(2):
                    recip_exp(tmpf[rcs[half]][:], [Rt[rcs[half]]])
                for half in range(2):
                    ob = bx[half]
                    rc = rcs[half]
                    ov = bank(ob).rearrange("p (h q) -> p h q", h=4)
                    rv = tmpf[rc][:].rearrange("p (h q) -> p h q", h=4)
                    dst = arena[:, (12 + 2 * half) * 512:(14 + 2 * half) * 512].rearrange(
                        "p (j t) -> p j t", j=2)[:, :, blk * 128:(blk + 1) * 128]
                    Rd = [Rar[12 + 2 * half], Rar[13 + 2 * half]]
                    tt(S.dve, dst[0:64], ov[0:64, 0:4:2, :], rv[64:128, 0:4:2, :], ALU.mult,
                       [Rb[ob], Rt[rc]], Rd)
                    tt(S.dve, dst[64:128], ov[64:128, 1:4:2, :], rv[0:64, 1:4:2, :], ALU.mult,
                       [Rb[ob], Rt[rc]], Rd)
                    TMP.free(rc)
                    ACCA.free(ob)
            drain(99)

            for mg in range(2):
                kpb = wload(HT["WpB"][mg])
                wpb = wslot[kpb][:].rearrange("p (k c) -> p k c", k=4)
                for ml in range(4):
                    m = mg * 4 + ml
                    pbk = ACC.get()
                    for kc in range(4):
                        mm(bank(pbk), wpb[:, kc, ml * 128:(ml + 1) * 128], arch(12 + kc),
                           kc == 0, kc == 3, [Rw[kpb], Rar[12 + kc]], [Rb[pbk]])
                    eg = TMP.get()
                    actf(tmpf[eg][:], arch(GBS[m]), AF.Exp, [Rar[GBS[m]]], [Rt[eg]], scale=-1.0)
                    actf(tmpf[eg][:], tmpf[eg][:], AF.Ln, [Rt[eg]], [Rt[eg]], bias=1.0)
                    actf(tmpf[eg][:], tmpf[eg][:], AF.Exp, [Rt[eg]], [Rt[eg]], scale=-1.0)
                    tt(S.dve, tmpf[eg][:], bank(pbk), tmpf[eg][:], ALU.mult, [Rb[pbk], Rt[eg]], [Rt[eg]])
                    ACC.free(pbk)
                    tt(S.dve, arch(16 + m), arch(16 + m), tmpf[eg][:], ALU.add, [Rar[16 + m], Rt[eg]], [Rar[16 + m]])
                    TMP.free(eg)
                WS.free(kpb)

            kws = [wload(t) for t in HT["Wout"]]
            w3s = [wslot[k][:].rearrange("p (k c) -> p k c", k=4) for k in kws]
            hn_js = []
            for s in range(4):
                bs = [ACC.get(), ACC.get()]
                for hf in range(2):
                    for kq in range(2):
                        k = kws[hf * 2 + kq]
                        for kc in range(4):
                            mm(bank(bs[hf]), arch(16 + kq * 4 + kc)[:, s * 128:(s + 1) * 128],
                               w3s[hf * 2 + kq][:, kc, :], kq == 0 and kc == 0, kq == 1 and kc == 3,
                               [Rw[k], Rar[16 + kq * 4 + kc]], [Rb[bs[hf]]])
                for hf in range(2):
                    hs = hbuf[:, s, hf * 512:(hf + 1) * 512]
                    tt(S.dve, hs, bank(bs[hf]), hs, ALU.add, [Rb[bs[hf]], Rh[s]], [Rh[s]])
                    ACC.free(bs[hf])
                si, rs = rstd_of(hbuf[:, s, :], [Rh[s]])
                j = TMP.get()
                tsmul(S.dve, tb(j)[:, 0:1024], hbuf[:, s, :], rs, [Rh[s], Rst[si]], [Rt[j]])
                STP.free(si)
                hn_js.append(j)
                if s >= 1:
                    transpose_to(tb(hn_js[s - 1]), [Rt[hn_js[s - 1]]], bufi, s - 1)
                    TMP.free(hn_js[s - 1])
            for k in kws:
                WS.free(k)
            transpose_to(tb(hn_js[3]), [Rt[hn_js[3]]], bufi, 3)
            TMP.free(hn_js[3])

            for tix, t in enumerate(HT["Wup"]):
                k = wload(t)
                w3 = wslot[k][:].rearrange("p (k c) -> p k c", k=8)
                for fl in range(2):
                    f = tix * 2 + fl
                    pool_ = ACC if f % 2 == 0 else ROT
                    zb = pool_.get()
                    for kc in range(8):
                        mm(bank(zb), w3[:, kc, fl * 128:(fl + 1) * 128], xnT[bufi][:, kc, :], kc == 0, kc == 7,
                           [Rw[k], RxnT[bufi]], [Rb[zb]])
                    r_ = TMP.get()
                    actf(tmpf[r_][:], bank(zb), AF.Relu, [Rb[zb]], [Rt[r_]])
                    pool_.free(zb)
                    tt(S.pool if f % 3 == 0 else S.dve, arch(f), tmpf[r_][:], tmpf[r_][:], ALU.mult,
                       [Rt[r_]], [Rar[f]])
                    TMP.free(r_)
                WS.free(k)

            if i + 2 < nt:
                carry = norm_stage(row_base + (i + 2) * T, on_pool=True)
            if i + 1 < nt:
                build_rope_tiles(i + 1)
                carry_q = wload(HT["qA"][0])
                q0gen = qA_gen(0, ROT, ROT, buf=(i + 1) % 2, out_ap=q0buf[:], Rout=[R("q0")], k=carry_q)
            else:
                q0gen = None

            for hf in range(2):
                bs = [ACC.get() for _ in range(4)]
                for kq in range(8):
                    k = wload(HT["Wdown"][hf * 8 + kq])
                    w3 = wslot[k][:].rearrange("p (k c) -> p k c", k=4)
                    for s in range(4):
                        for kc in range(4):
                            f = kq * 4 + kc
                            mm(bank(bs[s]), arch(f)[:, s * 128:(s + 1) * 128], w3[:, kc, :],
                               kq == 0 and kc == 0, kq == 7 and kc == 3, [Rw[k], Rar[f]], [Rb[bs[s]]])
                    WS.free(k)
                    if q0gen is not None and (hf * 8 + kq) >= 2 and (hf * 8 + kq) % 2 == 0:
                        try:
                            next(q0gen)
                        except StopIteration:
                            q0gen = None
                for s in range(4):
                    hs = hbuf[:, s, hf * 512:(hf + 1) * 512]
                    tt(S.dve, hs, bank(bs[s]), hs, ALU.add, [Rb[bs[s]], Rh[s]], [Rh[s]])
                    ACC.free(bs[s])

            if q0gen is not None:
                for _ in q0gen:
                    pass
                q0gen = None

            si, rss = rstd_batch([(hbuf[:, s, :], [Rh[s]]) for s in range(4)])
            for s in range(4):
                stt(S.dve, hbuf[:, s, :], hbuf[:, s, :], rss[s], gfb[:], ALU.mult, ALU.mult,
                    [Rh[s], Rst[si], R("gfb")], [Rh[s]])
                r0s = r0 + s * 128
                pdma(c_out[s], y_all[r0s:r0s + 128, :], hbuf[:, s, :], [Rh[s]], [])
            STP.free(si)

    row = 0
    for _ in cast_gen():
        pass
    for SL in seq_lens:
        process_sequence(row, SL, None)
        row += SL

    S.final_wait(S.sp, c_out)
    S.finalize_and_emit(st)
    st.close()
    return nc


def _stat8(W):
    return W.reshape(8, 128, 256).transpose(1, 0, 2).reshape(128, 2048)


def _k4(W):
    return W.reshape(4, 128, 512).transpose(1, 0, 2).reshape(128, 2048)


def _pack_weights(w_in, w_proj_a, w_proj_b, w_out, w_up, w_down):
    wt = np.empty((NH, 128, 2048), np.float32)
    perm = np.concatenate([np.r_[j * 64:(j + 1) * 64, (4 + j) * 64:(5 + j) * 64] for j in range(4)])
    qA = w_in[:, 0:512][:, perm]
    for h, t in enumerate(HT["qA"]):
        wt[t] = _stat8(qA[:, h * 256:(h + 1) * 256])
    wt[HT["kvA"][0]] = _stat8(w_in[:, 512:768])
    for h, t in enumerate(HT["qB"]):
        wt[t] = _stat8(w_in[:, 768 + h * 256:768 + (h + 1) * 256])
    for h, t in enumerate(HT["kB"]):
        wt[t] = _stat8(w_in[:, 1280 + h * 256:1280 + (h + 1) * 256])
    vB = w_in[:, 1792:2304]
    for q, t in enumerate(HT["vB"]):
        wt[t] = _k4(vB[q * 512:(q + 1) * 512, :])
    for h, t in enumerate(HT["gA"]):
        wt[t] = _stat8(w_in[:, 2304 + h * 256:2304 + (h + 1) * 256])
    for h, t in enumerate(HT["gB"]):
        wt[t] = _stat8(w_in[:, 3328 + h * 256:3328 + (h + 1) * 256])
    wpa = w_proj_a[perm, :]
    for h, t in enumerate(HT["WpA"]):
        wt[t] = _k4(wpa[:, h * 512:(h + 1) * 512])
    for h, t in enumerate(HT["WpB"]):
        wt[t] = _k4(w_proj_b[:, h * 512:(h + 1) * 512])
    for hf in range(2):
        for kq in range(2):
            wt[HT["Wout"][hf * 2 + kq]] = _k4(w_out[kq * 512:(kq + 1) * 512, hf * 512:(hf + 1) * 512])
    for h, t in enumerate(HT["Wup"]):
        wt[t] = _stat8(w_up[:, h * 256:(h + 1) * 256])
    for hf in range(2):
        for kq in range(8):
            wt[HT["Wdown"][hf * 8 + kq]] = _k4(w_down[kq * 512:(kq + 1) * 512, hf * 512:(hf + 1) * 512])
    return wt


def _constants():
    bf = ml_dtypes.bfloat16
    ident = np.eye(128, dtype=np.float32)
    bones = np.zeros((128, 128), np.float32)
    bones[0:64, 0:64] = 1.0 / 64
    bones[64:128, 64:128] = 1.0 / 64
    rotm = np.zeros((128, 128), np.float32)
    for i in range(64):
        rotm[2 * i + 1, 2 * i] = -1.0
        rotm[2 * i, 2 * i + 1] = 1.0
    cbf = np.concatenate([ident, bones, rotm], axis=1).astype(bf)
    inv = 10000.0 ** (-np.arange(0, 32, 2, dtype=np.float64) / 32.0)
    tabs = np.zeros((128, 4, 64), np.float64)
    pos = np.arange(64, dtype=np.float64)
    for p in range(128):
        i = (p % 64) // 2
        if i < 16:
            ang = pos * np.float64(np.float32(inv[i]))
            tabs[p, 0, :] = np.cos(ang)
            tabs[p, 1, :] = np.sin(ang)
        else:
            ang = pos * np.float64(np.float32(inv[i - 16]))
            tabs[p, 2, :] = np.cos(ang)
            tabs[p, 3, :] = np.sin(ang)
    tabs = tabs.astype(np.float32).reshape(128, 256)
    ind = np.zeros((32, 4096), np.float32)
    kc = np.arange(64)[:, None]
    qc = np.arange(64)[None, :]
    b = kc - qc + 15
    for bb in range(31):
        ind[bb] = (b == bb).astype(np.float32).reshape(-1)
    cstart = np.clip(qc - 8, 0, 48)
    cm = (kc >= cstart) & (kc < cstart + 16)
    ind[31] = np.where(cm, 0.0, -30000.0).astype(np.float32).reshape(-1)
    return cbf, tabs, ind


SEQ_LENS = [4096, 2048, 2048, 2048, 2048]
_CACHE = {}


def _shared_inputs(norm_mix_g, w_in, a_q_norm_g, a_k_norm_g, b_rel_pos_bias, w_proj_a, w_proj_b,
                   w_out, norm_mlp_g, w_mlp_up, w_mlp_down, norm_final_g):
    f = lambda a: np.ascontiguousarray(np.asarray(a, dtype=np.float32))
    wt = _pack_weights(f(w_in)[0], f(w_proj_a)[0], f(w_proj_b)[0], f(w_out)[0], f(w_mlp_up)[0],
                       f(w_mlp_down)[0])
    g1 = f(norm_mix_g)[0].reshape(8, 128).T
    g2 = f(norm_mlp_g)[0].reshape(8, 128).T
    gtabs = np.ascontiguousarray(np.concatenate([g1, g2], axis=1))
    smalls = np.ascontiguousarray(np.stack([np.tile(f(a_q_norm_g)[0], 2), np.tile(f(a_k_norm_g)[0], 2)], axis=1))
    gfb = np.ascontiguousarray(np.broadcast_to(f(norm_final_g)[None, :], (128, D)))
    rpb = f(b_rel_pos_bias)[0]
    rpbT = np.ones((32, 120), np.float32)
    rpbT[0:31] = rpb.reshape(120, 31).T
    cbf, tabs, ind = _constants()
    return {"wt": wt, "gtabs": gtabs, "smalls": smalls, "gfb": gfb, "rpbT": rpbT, "ind": ind,
            "cbf": cbf, "tabs": tabs}


def kernel(x_prompt, x_sample, norm_mix_g, w_in, a_q_norm_g, a_k_norm_g, b_rel_pos_bias,
           w_proj_a, w_proj_b, w_out, norm_mlp_g, w_mlp_up, w_mlp_down, norm_final_g):
    n = 8
    x_prompt = np.asarray(x_prompt, dtype=np.float32)
    x_sample = np.asarray(x_sample, dtype=np.float32)
    shared = _shared_inputs(norm_mix_g, w_in, a_q_norm_g, a_k_norm_g, b_rel_pos_bias, w_proj_a,
                            w_proj_b, w_out, norm_mlp_g, w_mlp_up, w_mlp_down, norm_final_g)
    nc = bass.Bass("TRN2", target_bir_lowering=False)
    build_program(nc, SEQ_LENS)
    in_maps = []
    for c in range(n):
        xa = np.concatenate([x_prompt[c].reshape(4096, D), x_sample[4 * c:4 * c + 4].reshape(8192, D)], axis=0)
        m = dict(shared)
        m["x_all"] = np.ascontiguousarray(xa)
        in_maps.append(m)
    res = run_bass_kernel_spmd(nc, in_maps, core_ids=list(range(n)))
    yp = np.empty((8, 4096, D), np.float32)
    ys = np.empty((32, 2048, D), np.float32)
    for c in range(n):
        y = res.results[c]["y_all"]
        yp[c] = y[0:4096]
        ys[4 * c:4 * c + 4] = y[4096:].reshape(4, 2048, D)
    return (yp, ys)
```

```python
import numpy as np
import ml_dtypes
from contextlib import ExitStack
import concourse.bass as bass
import concourse.mybir as mybir
from concourse.bass_utils import run_bass_kernel_spmd

F32 = mybir.dt.float32
BF16 = mybir.dt.bfloat16
ALU = mybir.AluOpType
AF = mybir.ActivationFunctionType

D = 1024
T = 512
EPS = 1e-6
QSCALE = 0.125
GRID_W = 64


class Res:
    __slots__ = ("name", "lw", "rd")

    def __init__(self, name):
        self.name = name
        self.lw = None
        self.rd = {}


class Lane:
    def __init__(self, name, is_dma=False):
        self.name = name
        self.is_dma = is_dma
        self.sem = None
        self.ops = []
        self.count = 0
        self.seen = {}


class Op:
    __slots__ = ("fn", "waits", "signal", "sigval", "chan")

    def __init__(self, fn):
        self.fn = fn
        self.waits = []
        self.signal = False
        self.sigval = 0
        self.chan = None


class _Noop:
    def then_inc(self, *a, **k):
        return self


class Sched:
    def __init__(self, nc):
        self.nc = nc
        self.pe = Lane("pe")
        self.act = Lane("act")
        self.dve = Lane("dve")
        self.pool = Lane("pool")
        self.sp = Lane("sp")
        self.engines = [self.pe, self.act, self.dve, self.pool, self.sp]
        self.chans = []

    def chan(self, name):
        c = Lane(name, is_dma=True)
        self.chans.append(c)
        return c

    def _deps(self, lane, reads, writes):
        deps = {}

        def add(l, i, same_ok):
            if l is lane and not same_ok:
                return
            if deps.get(l, -1) < i:
                deps[l] = i
        for r in reads:
            if r.lw is not None:
                add(r.lw[0], r.lw[1], True)
        for w in writes:
            if w.lw is not None:
                add(w.lw[0], w.lw[1], False)
            for l, i in w.rd.items():
                add(l, i, False)
        out = []
        for l, i in deps.items():
            if l.is_dma:
                i = l.count - 1
            if lane.seen.get(l, -1) >= i:
                continue
            lane.seen[l] = i
            out.append((l, i))
        return out

    def op(self, lane, fn, reads=(), writes=()):
        o = Op(fn)
        o.waits = self._deps(lane, reads, writes)
        idx = len(lane.ops)
        lane.ops.append(o)
        for r in reads:
            if r.rd.get(lane, -1) < idx:
                r.rd[lane] = idx
        for w in writes:
            w.lw = (lane, idx)
            w.rd = {}
        return o

    def dma(self, queue, chan, fn, reads=(), writes=()):
        o = Op(fn)
        o.chan = chan
        o.waits = self._deps(queue, reads, writes)
        queue.ops.append(o)
        idx = chan.count
        chan.count += 1
        for r in reads:
            if r.rd.get(chan, -1) < idx:
                r.rd[chan] = idx
        for w in writes:
            w.lw = (chan, idx)
            w.rd = {}
        return o

    def final_wait(self, lane, chans):
        o = Op(lambda e: _Noop())
        for c in chans:
            if c.count > 0:
                o.waits.append((c, c.count - 1))
        lane.ops.append(o)

    def finalize_and_emit(self, stack):
        nc = self.nc
        for l in self.engines + self.chans:
            l.sem = stack.enter_context(nc.semaphore("s_" + l.name))
        for l in self.engines:
            for o in l.ops:
                for (dl, di) in o.waits:
                    if not dl.is_dma:
                        dl.ops[di].signal = True
        for l in self.engines:
            v = 0
            for o in l.ops:
                if o.chan is None and o.signal:
                    v += 1
                o.sigval = v
        block = stack.enter_context(nc.Block())

        def emit(lane):
            def body(e):
                for o in lane.ops:
                    for (dl, di) in o.waits:
                        if dl.is_dma:
                            e.wait_ge(dl.sem, 16 * (di + 1))
                        else:
                            e.wait_ge(dl.sem, dl.ops[di].sigval)
                    ins = o.fn(e)
                    if o.chan is not None:
                        ins.then_inc(o.chan.sem, 16)
                    elif o.signal:
                        ins.then_inc(lane.sem, 1)
            return body
        block.tensor(emit(self.pe))
        block.scalar(emit(self.act))
        block.vector(emit(self.dve))
        block.gpsimd(emit(self.pool))
        block.sync(emit(self.sp))


class Pool_:
    def __init__(self, name, ids):
        self.name = name
        self.ids = list(ids)
        self.live = {i: False for i in self.ids}
        self.ptr = 0

    def get(self):
        n = len(self.ids)
        for k in range(n):
            i = self.ids[(self.ptr + k) % n]
            if not self.live[i]:
                self.live[i] = True
                self.ptr = (self.ptr + k + 1) % n
                return i
        raise RuntimeError("pool %s exhausted" % self.name)

    def get2(self):
        n = len(self.ids) // 2
        start = (self.ptr // 2)
        for k in range(n):
            j = (start + k) % n
            a, b = self.ids[2 * j], self.ids[2 * j + 1]
            if not self.live[a] and not self.live[b]:
                self.live[a] = self.live[b] = True
                self.ptr = (2 * j + 2) % len(self.ids)
                return a
        raise RuntimeError("pool %s exhausted (pair)" % self.name)

    def free(self, i, n=1):
        for k in range(n):
            assert self.live[i + k], (self.name, i + k)
            self.live[i + k] = False


HT = {}
_idx = 0
for _name, _n in [("qA", 2), ("kvA", 1), ("qB", 2), ("kB", 2), ("vB", 2), ("gA", 4), ("gB", 4),
                  ("WpA", 2), ("WpB", 2), ("Wout", 4), ("Wup", 16), ("Wdown", 16)]:
    HT[_name] = list(range(_idx, _idx + _n))
    _idx += _n
NH = _idx

SCALE_SPEC = {}
for _n in ["qA", "kvA", "qB", "kB", "gA", "gB"]:
    for _t in HT[_n]:
        SCALE_SPEC[_t] = (8, 0)
for _q, _t in enumerate(HT["vB"]):
    SCALE_SPEC[_t] = (4, 4 * _q)
for _t in HT["Wup"]:
    SCALE_SPEC[_t] = (8, 8)

RS = {0: (0, 0), 1: (0, 0), 2: (0, 1), 3: (2, 2), 4: (2, 2)}


def block_type(b, nblk):
    if b == 0:
        return 0, 0
    if b == 1:
        return 1, 0
    if b == nblk - 2:
        return 3, nblk - 5
    if b == nblk - 1:
        return 4, nblk - 5
    return 2, b - 2


def build_program(nc, seq_lens):
    NROWS = sum(seq_lens)
    S = Sched(nc)
    st = ExitStack()

    def din(name, shape, dt):
        return nc.dram_tensor(name, shape, dt, kind="ExternalInput").ap()

    x_all = din("x_all", [NROWS, D], F32)
    wt = din("wt", [NH, 128, 2048], F32)
    gtabs_d = din("gtabs", [128, 16], F32)
    smalls_d = din("smalls", [128, 2], F32)
    gf_d = din("gfb", [128, D], F32)
    rpbT_d = din("rpbT", [32, 120], F32)
    ind_d = din("ind", [32, 4096], F32)
    cbf_d = din("cbf", [128, 384], BF16)
    tabs_d = din("tabs", [128, 256], F32)
    y_all = nc.dram_tensor("y_all", [NROWS, D], F32, kind="ExternalOutput").ap()
    wsc = nc.dram_tensor("wsc", [NH, 128, 2048], BF16).ap()
    tz_d = nc.dram_tensor("tz_d", [120, 4096], F32).ap()
    e_d = nc.dram_tensor("e_d", [5, 128, 5120], BF16).ap()

    def sb(name, shape, dt):
        return st.enter_context(nc.sbuf_tensor(name, shape, dt))

    SMAX = max(max(seq_lens), 4096)
    NCH_MAX = SMAX // 128
    KA = sb("KA", [128, SMAX], BF16)
    VA = sb("VA", [128, NCH_MAX, 192], BF16)
    KBr = sb("KBr", [128, 4, 3, 512], BF16)
    VBr = sb("VBr", [128, 12, 4, 192], BF16)
    tabs = sb("tabss", [128, 4, 64], F32)
    Ct = sb("Ct", [128, 512], F32)
    St = sb("St", [128, 512], F32)
    Eint = sb("Eint", [128, 5120], BF16)
    Eedge = sb("Eedge", [128, 5120], BF16)
    NSLOT = 5
    wslot = [sb("wslot%d" % k, [128, 2048], BF16) for k in range(NSLOT)]
    wkvb = sb("wkvb", [128, 2048], BF16)
    hbuf = sb("hbuf", [128, 4, D], F32)
    xin = [sb("xin%d" % k, [128, D], F32) for k in range(2)]
    xnT = [sb("xnT%d" % k, [128, 8, 512], BF16) for k in range(2)]
    arena = sb("arena", [128, 32 * 512], BF16)
    pTb = [sb("pTb%d" % k, [128, 640], BF16) for k in range(3)]
    gfb = sb("gfbs", [128, D], F32)
    NTMP = 12
    tmpf = [sb("tmp%d" % k, [128, 512], F32) for k in range(NTMP)]
    cbf = sb("cbfs", [128, 384], BF16)
    gtabs = sb("gtabss", [128, 16], F32)
    smalls = sb("smallss", [128, 2], F32)
    stats = sb("stats", [128, 16 * 12], F32)
    rpbT = sb("rpbTs", [32, 120], F32)
    q0buf = sb("q0buf", [128, 512], BF16)
    ps = st.enter_context(nc.psum_tensor("ps", [128, 4096], F32))

    ident = cbf[:, 0:128]
    bones = cbf[:, 128:256]
    rotm = cbf[:, 256:384]

    RR = {}

    def R(name):
        r = RR.get(name)
        if r is None:
            r = RR[name] = Res(name)
        return r
    Rb = [R("bank%d" % b) for b in range(8)]
    Rw = [R("wslot%d" % k) for k in range(NSLOT)]
    Rt = [R("tmp%d" % k) for k in range(NTMP)]
    Rar = [R("ar%d" % k) for k in range(32)]
    Rh = [R("h%d" % s) for s in range(4)]
    Rxin = [R("xin%d" % k) for k in range(2)]
    RxnT = [R("xnT%d" % k) for k in range(2)]
    RKB = [R("KB%d" % k) for k in range(3)]
    RVB = [R("VB%d" % k) for k in range(3)]
    RpTb = [R("pTb%d" % k) for k in range(3)]
    Rst = [R("stat%d" % k) for k in range(16)]

    ACC = Pool_("acc", [0, 1, 2, 3])
    ROT = Pool_("rot", [4, 5, 6, 7])
    ACCA = Pool_("acca", [0, 1])
    FIL = Pool_("fil", [2, 3])
    TMP = Pool_("tmp", range(NTMP))
    WS = Pool_("ws", range(NSLOT))
    PTB = Pool_("ptb", range(3))
    PTA = Pool_("pta", range(3))
    STP = Pool_("stat", range(16))

    def bank(b, n=1):
        return ps[:, b * 512:(b + n) * 512]

    def tb(k):
        return tmpf[k][:].bitcast(BF16)

    def arch(k, n=1):
        return arena[:, k * 512:(k + n) * 512]

    c_set = S.chan("cset")
    c_wsc = [S.chan("cwsc%d" % k) for k in range(NSLOT)]
    c_tz = S.chan("ctz")
    c_tzk = S.chan("ctzk")
    c_ed = S.chan("ced")
    c_w = [S.chan("cw%d" % k) for k in range(NSLOT)]
    c_xin = [S.chan("cxin%d" % k) for k in range(2)]
    c_h = S.chan("chl")
    c_out = [S.chan("cout%d" % s) for s in range(4)]
    c_edge = S.chan("cedge")

    def mm(out, lhsT, rhs, start, stop, r, w):
        S.op(S.pe, lambda e: e.matmul(out, lhsT=lhsT, rhs=rhs, start=start, stop=stop), r, w)

    def actf(out, in_, func, r, w, scale=1.0, bias=0.0, accum=None):
        if accum is None:
            S.op(S.act, lambda e: e.activation(out=out, in_=in_, func=func, scale=scale, bias=bias), r, w)
        else:
            S.op(S.act, lambda e: e.activation(out=out, in_=in_, func=func, scale=scale, bias=bias,
                                               accum_out=accum), r, w)

    def copy_on(lane, out, in_, r, w):
        if lane is S.act:
            S.op(S.act, lambda e: e.activation(out=out, in_=in_, func=AF.Copy), r, w)
        else:
            S.op(lane, lambda e: e.tensor_copy(out=out, in_=in_), r, w)

    def tt(lane, out, in0, in1, op, r, w):
        S.op(lane, lambda e: e.tensor_tensor(out=out, in0=in0, in1=in1, op=op), r, w)

    def tsmul(lane, out, in0, scalar, r, w):
        S.op(lane, lambda e: e.tensor_scalar_mul(out=out, in0=in0, scalar1=scalar), r, w)

    def tsadd(lane, out, in0, scalar, r, w):
        S.op(lane, lambda e: e.tensor_scalar_add(out=out, in0=in0, scalar1=scalar), r, w)

    def stt(lane, out, in0, scalar, in1, op0, op1, r, w):
        S.op(lane, lambda e: e.scalar_tensor_tensor(out=out, in0=in0, scalar=scalar, in1=in1,
                                                    op0=op0, op1=op1), r, w)

    def recip_ln(out, in_, r, w, bias=0.0):
        actf(out, in_, AF.Ln, r, w, bias=bias)

    def recip_exp(buf, rw):
        actf(buf, buf, AF.Exp, rw, rw, scale=-1.0)

    def memset(lane, ap, val, w):
        S.op(lane, lambda e: e.memset(ap, val), (), w)

    def sdma(chan, out, in_, r, w):
        S.dma(S.sp, chan, lambda e: e.dma_start(out=out, in_=in_), r, w)

    def pdma(chan, out, in_, r, w):
        S.dma(S.pool, chan, lambda e: e.dma_start(out=out, in_=in_), r, w)

    rr_state = {"evac": 0}

    def evac_lane():
        rr_state["evac"] += 1
        return S.act if rr_state["evac"] % 3 == 0 else S.dve

    sdma(c_set, cbf[:], cbf_d, (), [R("cbf")])
    sdma(c_set, gtabs[:], gtabs_d, (), [R("gtabs")])
    sdma(c_set, smalls[:], smalls_d, (), [R("smalls")])
    sdma(c_set, gfb[:], gf_d, (), [R("gfb")])
    sdma(c_set, tabs[:].rearrange("p a b -> p (a b)"), tabs_d, (), [R("tabs")])
    sdma(c_set, rpbT[:], rpbT_d, (), [R("rpbT")])
    hst = hbuf[:].rearrange("p s d -> p (s d)")
    arena_f = arena[:].bitcast(F32)
    stage_f = [(hst[:, 0:2048], [Rh[0], Rh[1]]), (hst[:, 2048:4096], [Rh[2], Rh[3]])]
    for q in range(4):
        stage_f.append((arena_f[:, q * 2048:(q + 1) * 2048], Rar[8 * q:8 * q + 8]))
    c_wst = [S.chan("cwst%d" % k) for k in range(len(stage_f))]
    cast_state = {"n": 0}

    def cast_load(t):
        fs = cast_state["n"] % len(stage_f)
        cast_state["n"] += 1
        src, Rsrc = stage_f[fs]
        sdma(c_wst[fs], src, wt[t], (), Rsrc)
        return fs

    def cast_one(t, fs=None):
        if fs is None:
            fs = cast_load(t)
        src, Rsrc = stage_f[fs]
        is_kv = (t == HT["kvA"][0])
        if is_kv:
            dst, Rdst, k = wkvb[:], [R("wkvb")], None
        else:
            k = WS.get()
            dst, Rdst = wslot[k][:], [Rw[k]]
        spec = SCALE_SPEC.get(t)
        lane = S.act if t % 2 == 0 else S.dve
        if spec is None:
            copy_on(lane, dst, src, Rsrc, Rdst)
        else:
            nkc, c0 = spec
            wdt = 2048 // nkc
            for kc in range(nkc):
                o_ = dst[:, kc * wdt:(kc + 1) * wdt]
                i_ = src[:, kc * wdt:(kc + 1) * wdt]
                sc = gtabs[:, c0 + kc:c0 + kc + 1]
                if lane is S.act:
                    actf(o_, i_, AF.Copy, Rsrc + [R("gtabs")], Rdst, scale=sc)
                else:
                    tsmul(S.dve, o_, i_, sc, Rsrc + [R("gtabs")], Rdst)
        if not is_kv:
            pdma(c_wsc[k], wsc[t], dst, Rdst, [R("wsc%d" % t)])
            WS.free(k)

    def cast_gen():
        ts = [t for t in range(NH) if t != HT["kvA"][0]]
        PF = len(stage_f) - 1
        pend = []
        for q, t in enumerate(ts):
            while len(pend) < PF and len(pend) + q < len(ts):
                t2 = ts[q + len(pend)]
                pend.append((t2, cast_load(t2)))
            t1, fs = pend.pop(0)
            assert t1 == t
            cast_one(t, fs)
            yield

    cast_one(HT["kvA"][0])

    ind_sb = arena_f[0:32, 0:4096]
    tz_sb = arena_f[0:120, 4096:8192]
    sdma(c_set, ind_sb, ind_d, (), Rar)
    for n in range(8):
        mm(bank(n)[0:120, :], rpbT[:, :], ind_sb[:, n * 512:(n + 1) * 512], True, True,
           Rar + [R("rpbT")], [Rb[n]])
        copy_on(S.dve if n % 2 else S.act, tz_sb[:, n * 512:(n + 1) * 512], bank(n)[0:120, :], [Rb[n]], Rar)
    sdma(c_tz, tz_d, tz_sb, Rar, [R("tz_d")])
    VBflat = VBr[:].rearrange("p a b c -> p (a b c)")
    ez = [VBflat[:, 0:3840], VBflat[:, 4608:8448]]
    for grp in range(2):
        for kr2 in range(2):
            sdma(c_tzk,
                 hst[kr2 * 64:(kr2 + 1) * 64, 0:3840].rearrange("p (a q) -> p a q", q=64),
                 tz_d[grp * 60:(grp + 1) * 60, :].rearrange("a (k q) -> k a q", q=64),
                 [R("tz_d")], Rh)
        actf(ez[grp], hst[:, 0:3840], AF.Copy, Rh, RVB, scale=1.0 / QSCALE)

    def build_E(ty, dst, Rd, lane):
        dv = dst.rearrange("p (h c q) -> p h c q", h=8, c=5)
        for ch in range(5):
            for kr2 in range(2):
                for qr2 in range(2):
                    krow = 2 * ch + kr2
                    rs = RS[ty][qr2]
                    if not (rs <= krow < rs + 8):
                        continue
                    a_ = krow - (2 * ty + qr2) + 7
                    assert 0 <= a_ <= 14
                    for grp in range(2):
                        i_ = ez[grp][kr2 * 64:(kr2 + 1) * 64, :].rearrange(
                            "p (h a q) -> p h a q", h=4, a=15)[:, :, a_, :]
                        o_ = dv[kr2 * 64:(kr2 + 1) * 64, grp * 4:(grp + 1) * 4, ch, qr2 * 64:(qr2 + 1) * 64]
                        copy_on(lane, o_, i_, RVB, Rd)

    VAflat = VA[:].rearrange("p a b -> p (a b)")
    KBflat = KBr[:].rearrange("p a b c -> p (a b c)")
    stagings = {0: (VAflat[:, 0:5120], [R("VA")], S.dve), 1: (KBflat[:, 0:5120], RKB, S.act),
                3: (Eedge[:], [R("Eedge")], S.act), 4: (arena[:, 0:5120], Rar[0:10], S.pool)}
    stagings[2] = (Eint[:], [R("Eint")], S.dve)
    for n_, ty in enumerate((2, 0, 1, 3, 4)):
        dst, Rd, lane = stagings[ty]
        memset(S.pool if n_ % 2 else S.dve, dst, -240000.0, Rd)
    for ty in (2, 1, 4, 0, 3):
        dst, Rd, lane = stagings[ty]
        build_E(ty, dst, Rd, lane)
        if ty != 2:
            sdma(c_ed, e_d[ty], dst, Rd, [R("e_d")])
    memset(S.pool, VA[:, :, 64:128], 1.0, [R("VA")])
    memset(S.pool, VBr[:, :, :, 64:128], 1.0, RVB)

    def wload(t, on_pool=False):
        k = WS.get()
        (pdma if on_pool else sdma)(c_w[k], wslot[k][:], wsc[t], [R("wsc%d" % t)], [Rw[k]])
        return k

    def rstd_batch(srcs):
        si = STP.get()
        n = len(srcs)
        b0 = 12 * si
        j = TMP.get()
        for q, (src_ap, Rsrc) in enumerate(srcs):
            actf(tb(j)[:, 0:1024], src_ap, AF.Square, Rsrc, [Rt[j], Rst[si]], accum=stats[:, b0 + q:b0 + q + 1])
        TMP.free(j)
        actf(stats[:, b0 + 4:b0 + 4 + n], stats[:, b0:b0 + n], AF.Ln, [Rst[si]], [Rst[si]], scale=1.0 / D, bias=EPS)
        actf(stats[:, b0 + 8:b0 + 8 + n], stats[:, b0 + 4:b0 + 4 + n], AF.Exp, [Rst[si]], [Rst[si]], scale=-0.5)
        return si, [stats[:, b0 + 8 + q:b0 + 9 + q] for q in range(n)]

    def rstd_of(src_ap, Rsrc):
        si, cols = rstd_batch([(src_ap, Rsrc)])
        return si, cols[0]

    def transpose_to(src_bf, Rsrc, dst_buf, s):
        b = ROT.get2()
        for kc in range(8):
            mm(bank(b, 2)[:, kc * 128:(kc + 1) * 128], src_bf[:, kc * 128:(kc + 1) * 128], ident,
               True, True, Rsrc + [R("cbf")], [Rb[b], Rb[b + 1]])
        copy_on(evac_lane(), xnT[dst_buf][:, :, s * 128:(s + 1) * 128],
                bank(b, 2).rearrange("p (k t) -> p k t", k=8), [Rb[b], Rb[b + 1]], [RxnT[dst_buf]])
        ROT.free(b, 2)

    xin_rr = {"k": 0}

    def norm_stage(row0, on_pool=False):
        js = []
        for s in range(4):
            k = xin_rr["k"]
            xin_rr["k"] = 1 - k
            r0 = row0 + s * 128
            (pdma if on_pool else sdma)(c_xin[k], xin[k][:], x_all[r0:r0 + 128, :], (), [Rxin[k]])
            si, rs = rstd_of(xin[k][:], [Rxin[k]])
            j = TMP.get()
            tsmul(S.dve, tb(j)[:, 0:1024], xin[k][:], rs, [Rxin[k], Rst[si]], [Rt[j]])
            STP.free(si)
            js.append(j)
        return js

    def transpose_stage(js, dst_buf):
        for s, j in enumerate(js):
            transpose_to(tb(j), [Rt[j]], dst_buf, s)
            TMP.free(j)

    def make_xnT(row0, dst_buf):
        transpose_stage(norm_stage(row0), dst_buf)

    def build_rope_tiles(tile):
        row0 = tile * 8
        for (dst, Rd, a) in ((Ct, R("Ct"), 0), (St, R("St"), 1)):
            S.op(S.pool, (lambda dst, a: lambda e: e.tensor_tensor(
                out=dst[:].rearrange("p (r c) -> p r c", c=64),
                in0=tabs[:, a, row0:row0 + 8].unsqueeze(2).to_broadcast([128, 8, 64]),
                in1=tabs[:, 2 + a, :].unsqueeze(1).to_broadcast([128, 8, 64]),
                op=ALU.add))(dst, a), [R("tabs")], [Rd])

    def headnorm_rope(zb, zpool, gcol, out_ap, Rout, spool=None):
        spool = spool or ROT
        sq = TMP.get()
        actf(tb(sq)[:, 0:512], bank(zb), AF.Square, [Rb[zb]], [Rt[sq]])
        yield
        sbk = spool.get()
        mm(bank(sbk), bones, tb(sq)[:, 0:512], True, True, [Rt[sq], R("cbf")], [Rb[sbk]])
        TMP.free(sq)
        ln = TMP.get()
        actf(tmpf[ln][:], bank(sbk), AF.Ln, [Rb[sbk]], [Rt[ln]], bias=EPS)
        spool.free(sbk)
        actf(tmpf[ln][:], tmpf[ln][:], AF.Exp, [Rt[ln]], [Rt[ln]], scale=-0.5)
        kn = TMP.get()
        stt(S.dve, tb(kn)[:, 0:512], bank(zb), gcol, tmpf[ln][:], ALU.mult, ALU.mult,
            [Rb[zb], Rt[ln], R("smalls")], [Rt[kn]])
        zpool.free(zb)
        TMP.free(ln)
        yield
        rb = spool.get()
        mm(bank(rb), rotm, tb(kn)[:, 0:512], True, True, [Rt[kn], R("cbf")], [Rb[rb]])
        t1 = TMP.get()
        ew = S.dve if spool is FIL else S.pool
        tt(ew, tmpf[t1][:], tb(kn)[:, 0:512], Ct[:], ALU.mult, [Rt[kn], R("Ct")], [Rt[t1]])
        TMP.free(kn)
        t2 = TMP.get()
        tt(S.dve, tmpf[t2][:], bank(rb), St[:], ALU.mult, [Rb[rb], R("St")], [Rt[t2]])
        spool.free(rb)
        tt(ew, out_ap, tmpf[t1][:], tmpf[t2][:], ALU.add, [Rt[t1], Rt[t2]], Rout)
        TMP.free(t1)
        TMP.free(t2)
        yield

    def run_jobs(jobs, lag=1):
        active = []
        jobs = list(jobs)
        while jobs or active:
            if jobs:
                active.append(jobs.pop(0))
            nxt = []
            for g in active:
                try:
                    next(g)
                    nxt.append(g)
                except StopIteration:
                    pass
            active = nxt

    def process_sequence(row_base, SL, extra_gen=None):
        nt = SL // T
        nch = SL // 128
        nblk = nch
        wkv3 = wkvb[:].rearrange("p (k c) -> p k c", k=8)

        def p1_job(tile):
            buf = tile % 2
            js = norm_stage(row_base + tile * T)
            yield
            transpose_stage(js, buf)
            yield
            zb = ACC.get()
            for kc in range(8):
                mm(bank(zb), wkv3[:, kc, 0:128], xnT[buf][:, kc, :], kc == 0, kc == 7,
                   [R("wkvb"), RxnT[buf]], [Rb[zb]])
            for s in range(4):
                vb_ = ROT.get()
                for kc in range(8):
                    mm(bank(vb_)[:, 0:128], xnT[buf][:, kc, s * 128:(s + 1) * 128], wkv3[:, kc, 128:256],
                       kc == 0, kc == 7, [R("wkvb"), RxnT[buf]], [Rb[vb_]])
                c = tile * 4 + s
                copy_on(evac_lane(), VA[:, c, :].rearrange("p (a d) -> p a d", a=3)[:, 0:3:2, :],
                        bank(vb_)[:, 0:128].rearrange("p (a d) -> p a d", a=2), [Rb[vb_]], [R("VA")])
                ROT.free(vb_)
            hn = headnorm_rope(zb, ACC, smalls[:, 1:2], KA[:, tile * T:(tile + 1) * T], [R("KA")])
            next(hn)
            yield
            next(hn)
            yield
            build_rope_tiles(tile)
            for _ in hn:
                pass

        jobs = [p1_job(t) for t in range(nt)]
        active = []
        while jobs or active:
            if jobs:
                active.append(jobs.pop(0))
            nxt_ = []
            for g in active:
                try:
                    next(g)
                    nxt_.append(g)
                except StopIteration:
                    pass
            active = nxt_
            if extra_gen is not None:
                for _ in range(6):
                    try:
                        next(extra_gen)
                    except StopIteration:
                        extra_gen = None
                        break
        if extra_gen is not None:
            for _ in extra_gen:
                pass

        def proj_kvB(tile):
            buf = tile % 2
            slot = tile % 3
            for half, t in enumerate(HT["kB"]):
                k = wload(t)
                w3 = wslot[k][:].rearrange("p (k c) -> p k c", k=8)
                for jl in range(2):
                    jj = half * 2 + jl
                    zb = ROT.get()
                    for kc in range(8):
                        mm(bank(zb), w3[:, kc, jl * 128:(jl + 1) * 128], xnT[buf][:, kc, :], kc == 0, kc == 7,
                           [Rw[k], RxnT[buf]], [Rb[zb]])
                    copy_on(evac_lane(), KBr[:, jj, slot, :], bank(zb), [Rb[zb]], [RKB[slot]])
                    ROT.free(zb)
                WS.free(k)
            bs = [ACC.get() for _ in range(4)]
            for kq, t in enumerate(HT["vB"]):
                k = wload(t)
                w3 = wslot[k][:].rearrange("p (k c) -> p k c", k=4)
                for s in range(4):
                    for kc in range(4):
                        mm(bank(bs[s]), xnT[buf][:, kq * 4 + kc, s * 128:(s + 1) * 128], w3[:, kc, :],
                           kq == 0 and kc == 0, kq == 1 and kc == 3, [Rw[k], RxnT[buf]], [Rb[bs[s]]])
                WS.free(k)
            for s in range(4):
                c = slot * 4 + s
                copy_on(evac_lane(),
                        VBr[:, c, :, :].rearrange("p j (a d) -> p j a d", a=3)[:, :, 0:3:2, :],
                        bank(bs[s]).rearrange("p (j a d) -> p j a d", j=4, a=2), [Rb[bs[s]]], [RVB[slot]])
                ACC.free(bs[s])

        make_xnT(row_base, 0)
        proj_kvB(0)
        carry = norm_stage(row_base + T) if nt > 1 else None
        carry_q = None
        for i in range(nt):
            bufi = i % 2
            nxt_js = carry
            carry = None

            wq = {}

            def qw(name, idx, nuse):
                key = (name, idx)
                if key not in wq:
                    wq[key] = [wload(HT[name][idx]), nuse]
                ent = wq[key]
                return ent[0]

            def qw_done(name, idx):
                ent = wq[(name, idx)]
                ent[1] -= 1
                if ent[1] == 0:
                    WS.free(ent[0])

            def qA_gen(j, zpool, spool, buf=None, out_ap=None, Rout=None, k=None):
                buf = bufi if buf is None else buf
                own = k is None
                if own:
                    k = qw("qA", j // 2, 2)
                w3 = wslot[k][:].rearrange("p (k c) -> p k c", k=8)
                jl = j % 2
                zb = zpool.get()
                for kc in range(8):
                    mm(bank(zb), w3[:, kc, jl * 128:(jl + 1) * 128], xnT[buf][:, kc, :], kc == 0, kc == 7,
                       [Rw[k], RxnT[buf]], [Rb[zb]])
                    if kc in (1, 3, 5):
                        yield
                if own:
                    qw_done("qA", j // 2)
                yield from headnorm_rope(zb, zpool, smalls[:, 0:1], arch(j) if out_ap is None else out_ap,
                                         [Rar[j]] if Rout is None else Rout, spool)

            def qB_gen(jj):
                k = qw("qB", jj // 2, 2)
                w3 = wslot[k][:].rearrange("p (k c) -> p k c", k=8)
                jl = jj % 2
                zb = FIL.get()
                for kc in range(8):
                    mm(bank(zb), w3[:, kc, jl * 128:(jl + 1) * 128], xnT[bufi][:, kc, :], kc == 0, kc == 7,
                       [Rw[k], RxnT[bufi]], [Rb[zb]])
                    if kc in (1, 3, 5):
                        yield
                qw_done("qB", jj // 2)
                copy_on(S.dve, arch(4 + jj), bank(zb), [Rb[zb]], [Rar[4 + jj]])
                FIL.free(zb)
                yield

            def T_gen(s_, j_, dst_buf):
                b_ = FIL.get2()
                for kc in range(8):
                    mm(bank(b_, 2)[:, kc * 128:(kc + 1) * 128], tb(j_)[:, kc * 128:(kc + 1) * 128], ident,
                       True, True, [Rt[j_], R("cbf")], [Rb[b_], Rb[b_ + 1]])
                    if kc == 3:
                        yield
                TMP.free(j_)
                copy_on(S.dve, xnT[dst_buf][:, :, s_ * 128:(s_ + 1) * 128],
                        bank(b_, 2).rearrange("p (k t) -> p k t", k=8), [Rb[b_], Rb[b_ + 1]], [RxnT[dst_buf]])
                FIL.free(b_, 2)
                yield

            def kB_gen(jj, tile):
                buf = tile % 2
                k = qw("kB", jj // 2, 2)
                w3 = wslot[k][:].rearrange("p (k c) -> p k c", k=8)
                jl = jj % 2
                zb = FIL.get()
                for kc in range(8):
                    mm(bank(zb), w3[:, kc, jl * 128:(jl + 1) * 128], xnT[buf][:, kc, :], kc == 0, kc == 7,
                       [Rw[k], RxnT[buf]], [Rb[zb]])
                    if kc in (1, 3, 5):
                        yield
                qw_done("kB", jj // 2)
                copy_on(S.dve, KBr[:, jj, tile % 3, :], bank(zb), [Rb[zb]], [RKB[tile % 3]])
                FIL.free(zb)
                yield

            def vB_gen(s_, tile):
                buf = tile % 2
                slot = tile % 3
                zb = FIL.get()
                for kq in range(2):
                    k = qw("vB", kq, 4)
                    w3 = wslot[k][:].rearrange("p (k c) -> p k c", k=4)
                    for kc in range(4):
                        mm(bank(zb), xnT[buf][:, kq * 4 + kc, s_ * 128:(s_ + 1) * 128], w3[:, kc, :],
                           kq == 0 and kc == 0, kq == 1 and kc == 3, [Rw[k], RxnT[buf]], [Rb[zb]])
                        if kc == 1:
                            yield
                    qw_done("vB", kq)
                    if kq == 0:
                        yield
                c_ = slot * 4 + s_
                copy_on(S.dve,
                        VBr[:, c_, :, :].rearrange("p j (a d) -> p j a d", a=3)[:, :, 0:3:2, :],
                        bank(zb).rearrange("p (j a d) -> p j a d", j=4, a=2), [Rb[zb]], [RVB[slot]])
                FIL.free(zb)
                yield

            def gA_gen(m):
                k = qw("gA", m // 2, 2)
                w3 = wslot[k][:].rearrange("p (k c) -> p k c", k=8)
                ml = m % 2
                zb = FIL.get()
                for kc in range(8):
                    mm(bank(zb), w3[:, kc, ml * 128:(ml + 1) * 128], xnT[bufi][:, kc, :], kc == 0, kc == 7,
                       [Rw[k], RxnT[bufi]], [Rb[zb]])
                    if kc in (1, 3, 5):
                        yield
                qw_done("gA", m // 2)
                copy_on(S.dve, arch(16 + m), bank(zb), [Rb[zb]], [Rar[16 + m]])
                FIL.free(zb)
                yield

            if i == 0:
                build_rope_tiles(0)
                for _ in qA_gen(0, ACC, ROT, out_ap=q0buf[:], Rout=[R("q0")]):
                    pass
            else:
                wq[("qA", 0)] = [carry_q, 1]
                carry_q = None
            fillers = []
            for j in (1, 2, 3):
                fillers.append((j, qA_gen(j, FIL, FIL)))
            for jj in range(4):
                fillers.append((99, qB_gen(jj)))
            if nxt_js is not None:
                for s_ in range(4):
                    fillers.append((99, T_gen(s_, nxt_js[s_], (i + 1) % 2)))
                for jj in range(4):
                    fillers.append((99, kB_gen(jj, i + 1)))
                for s_ in range(4):
                    fillers.append((99, vB_gen(s_, i + 1)))
            for m in range(8):
                fillers.append((99, gA_gen(m)))
            n_units_est = 3 * 7 + 4 * 5 + (8 + 20 + 20 if nxt_js is not None else 0) + 40
            fstate = {"done": 0}

            def pump(n=1):
                while n > 0 and fillers:
                    try:
                        next(fillers[0][1])
                        fstate["done"] += 1
                        n -= 1
                    except StopIteration:
                        fillers.pop(0)

            def drain(deadline):
                while fillers and fillers[0][0] <= deadline:
                    try:
                        next(fillers[0][1])
                        fstate["done"] += 1
                    except StopIteration:
                        fillers.pop(0)

            r0 = row_base + i * T
            pdma(c_h, hbuf[:], x_all[r0:r0 + T, :].rearrange("(s p) d -> p s d", p=128), (), Rh)

            tot_steps = 4 * nch
            for j in range(4):
                qsrc = q0buf[:] if j == 0 else arch(j)
                Rq = R("q0") if j == 0 else Rar[j]
                drain(j)
                accA = ACCA.get()
                accB = ACCA.get()
                prev = None
                for c in range(nch + 1):
                    cur = None
                    step = j * nch + c
                    while fillers and fstate["done"] < (step + 1) * n_units_est / tot_steps:
                        pump(1)
                    if c < nch:
                        r2 = ROT.get2()
                        mm(bank(r2), KA[0:64, c * 128:(c + 1) * 128], qsrc[0:64, :], True, True,
                           [R("KA"), Rq], [Rb[r2]])
                        mm(bank(r2 + 1), KA[64:128, c * 128:(c + 1) * 128], qsrc[64:128, :], True, True,
                           [R("KA"), Rq], [Rb[r2 + 1]])
                        pt = PTA.get()
                        actf(arch(24 + 2 * pt, 2), bank(r2, 2), AF.Exp, [Rb[r2], Rb[r2 + 1]],
                             [Rar[24 + 2 * pt], Rar[25 + 2 * pt]], scale=QSCALE)
                        ROT.free(r2, 2)
                        cur = (c, pt)
                    if prev is not None:
                        pc, ppt = prev
                        mm(bank(accA), VA[:, pc, 0:128], arch(24 + 2 * ppt), pc == 0, pc == nch - 1,
                           [R("VA"), Rar[24 + 2 * ppt]], [Rb[accA]])
                        mm(bank(accB), VA[:, pc, 64:192], arch(25 + 2 * ppt), pc == 0, pc == nch - 1,
                           [R("VA"), Rar[25 + 2 * ppt]], [Rb[accB]])
                        PTA.free(ppt)
                    prev = cur
                rc = TMP.get()
                rc2 = TMP.get()
                recip_ln(tmpf[rc][64:128, :], bank(accA)[64:128, :], [Rb[accA]], [Rt[rc]])
                recip_ln(tmpf[rc2][0:64, :], bank(accB)[0:64, :], [Rb[accB]], [Rt[rc2]])
                recip_exp(tmpf[rc][64:128, :], [Rt[rc]])
                recip_exp(tmpf[rc2][0:64, :], [Rt[rc2]])
                tt(S.dve, arch(8 + j)[0:64, :], bank(accA)[0:64, :], tmpf[rc][64:128, :], ALU.mult,
                   [Rb[accA], Rt[rc]], [Rar[8 + j]])
                tt(S.dve, arch(8 + j)[64:128, :], bank(accB)[64:128, :], tmpf[rc2][0:64, :], ALU.mult,
                   [Rb[accB], Rt[rc2]], [Rar[8 + j]])
                TMP.free(rc)
                TMP.free(rc2)
                ACCA.free(accA)
                ACCA.free(accB)
            drain(99)

            GBS = [0, 1, 2, 3, 24, 25, 26, 27]

            def tA_gen(m):
                k = qw("WpA", m // 4, 4)
                wp = wslot[k][:].rearrange("p (k c) -> p k c", k=4)
                ml = m % 4
                pbk = FIL.get()
                for kc in range(4):
                    mm(bank(pbk), wp[:, kc, ml * 128:(ml + 1) * 128], arch(8 + kc), kc == 0, kc == 3,
                       [Rw[k], Rar[8 + kc]], [Rb[pbk]])
                    if kc == 1:
                        yield
                qw_done("WpA", m // 4)
                eg = TMP.get()
                actf(tmpf[eg][:], arch(16 + m), AF.Exp, [Rar[16 + m]], [Rt[eg]], scale=-1.0)
                actf(tmpf[eg][:], tmpf[eg][:], AF.Ln, [Rt[eg]], [Rt[eg]], bias=1.0)
                yield
                actf(tmpf[eg][:], tmpf[eg][:], AF.Exp, [Rt[eg]], [Rt[eg]], scale=-1.0)
                tt(S.dve, arch(16 + m), bank(pbk), tmpf[eg][:], ALU.mult, [Rb[pbk], Rt[eg]], [Rar[16 + m]])
                FIL.free(pbk)
                TMP.free(eg)
                yield

            def gB_gen(m):
                k = qw("gB", m // 2, 2)
                w3 = wslot[k][:].rearrange("p (k c) -> p k c", k=8)
                ml = m % 2
                zb = FIL.get()
                for kc in range(8):
                    mm(bank(zb), w3[:, kc, ml * 128:(ml + 1) * 128], xnT[bufi][:, kc, :], kc == 0, kc == 7,
                       [Rw[k], RxnT[bufi]], [Rb[zb]])
                    if kc in (1, 3, 5):
                        yield
                qw_done("gB", m // 2)
                copy_on(S.dve, arch(GBS[m]), bank(zb), [Rb[zb]], [Rar[GBS[m]]])
                FIL.free(zb)
                yield

            for m in range(8):
                fillers.append((99, tA_gen(m)))
            for m in range(8):
                fillers.append((99, gB_gen(m)))
            n_units_b = 8 * 4 + 8 * 5
            fstate["done"] = 0

            for blk in range(4):
                b = i * 4 + blk
                ty, cs0 = block_type(b, nblk)
                if ty == 2:
                    E, RE = Eint, R("Eint")
                else:
                    sdma(c_edge, Eedge[:], e_d[ty], [R("e_d")], [R("Eedge")])
                    E, RE = Eedge, R("Eedge")
                Ev = E[:].rearrange("p (h f) -> p h f", h=8)
                bx = [ACCA.get(), ACCA.get()]
                prev = None
                for h in range(9):
                    cur = None
                    while fillers and fstate["done"] < (blk * 9 + h + 1) * n_units_b / 36.0:
                        pump(1)
                    if h < 8:
                        jj, hp = h // 2, h % 2
                        r2 = ROT.get2()
                        mm(bank(r2, 2)[:, 0:512], ident, Ev[:, h, 0:512], True, False,
                           [RE, R("cbf")], [Rb[r2], Rb[r2 + 1]])
                        mm(bank(r2, 2)[:, 512:640], ident, Ev[:, h, 512:640], True, False,
                           [RE, R("cbf")], [Rb[r2], Rb[r2 + 1]])
                        for ch in range(5):
                            cs = cs0 + ch
                            tt_, s_ = cs // 4, cs % 4
                            mm(bank(r2, 2)[:, ch * 128:(ch + 1) * 128],
                               KBr[hp * 64:(hp + 1) * 64, jj, tt_ % 3, s_ * 128:(s_ + 1) * 128],
                               arch(4 + jj)[hp * 64:(hp + 1) * 64, blk * 128:(blk + 1) * 128],
                               False, ch in (3, 4),
                               [RKB[tt_ % 3], Rar[4 + jj]], [Rb[r2], Rb[r2 + 1]])

                        pt = PTB.get()
                        actf(pTb[pt][:], bank(r2, 2)[:, 0:640], AF.Exp, [Rb[r2], Rb[r2 + 1]], [RpTb[pt]],
                             scale=QSCALE)
                        ROT.free(r2, 2)
                        cur = (h, pt)
                    if prev is not None:
                        ph, ppt = prev
                        jj, hp = ph // 2, ph % 2
                        ob = bx[ph // 4]
                        for ch in range(5):
                            cs = cs0 + ch
                            tt_, s_ = cs // 4, cs % 4
                            mm(bank(ob)[:, (ph % 4) * 128:(ph % 4 + 1) * 128],
                               VBr[:, (tt_ % 3) * 4 + s_, jj, hp * 64:hp * 64 + 128],
                               pTb[ppt][:, ch * 128:(ch + 1) * 128], ch == 0, ch == 4,
                               [RVB[tt_ % 3], RpTb[ppt]], [Rb[ob]])
                        PTB.free(ppt)
                    prev = cur
                rcs = [TMP.get(), TMP.get()]
                for half in range(2):
                    recip_ln(tmpf[rcs[half]][:], bank(bx[half]), [Rb[bx[half]]], [Rt[rcs[half]]])
                for half in range(2):
                    recip_exp(tmpf[rcs[half]][:], [Rt[rcs[half]]])
                for half in range(2):
                    ob = bx[half]
                    rc = rcs[half]
                    ov = bank(ob).rearrange("p (h q) -> p h q", h=4)
                    rv = tmpf[rc][:].rearrange("p (h q) -> p h q", h=4)
                    dst = arena[:, (12 + 2 * half) * 512:(14 + 2 * half) * 512].rearrange(
                        "p (j t) -> p j t", j=2)[:, :, blk * 128:(blk + 1) * 128]
                    Rd = [Rar[12 + 2 * half], Rar[13 + 2 * half]]
                    tt(S.dve, dst[0:64], ov[0:64, 0:4:2, :], rv[64:128, 0:4:2, :], ALU.mult,
                       [Rb[ob], Rt[rc]], Rd)
                    tt(S.dve, dst[64:128], ov[64:128, 1:4:2, :], rv[0:64, 1:4:2, :], ALU.mult,
                       [Rb[ob], Rt[rc]], Rd)
                    TMP.free(rc)
                    ACCA.free(ob)
            drain(99)

            for mg in range(2):
                kpb = wload(HT["WpB"][mg])
                wpb = wslot[kpb][:].rearrange("p (k c) -> p k c", k=4)
                for ml in range(4):
                    m = mg * 4 + ml
                    pbk = ACC.get()
                    for kc in range(4):
                        mm(bank(pbk), wpb[:, kc, ml * 128:(ml + 1) * 128], arch(12 + kc),
                           kc == 0, kc == 3, [Rw[kpb], Rar[12 + kc]], [Rb[pbk]])
                    eg = TMP.get()
                    actf(tmpf[eg][:], arch(GBS[m]), AF.Exp, [Rar[GBS[m]]], [Rt[eg]], scale=-1.0)
                    actf(tmpf[eg][:], tmpf[eg][:], AF.Ln, [Rt[eg]], [Rt[eg]], bias=1.0)
                    actf(tmpf[eg][:], tmpf[eg][:], AF.Exp, [Rt[eg]], [Rt[eg]], scale=-1.0)
                    tt(S.dve, tmpf[eg][:], bank(pbk), tmpf[eg][:], ALU.mult, [Rb[pbk], Rt[eg]], [Rt[eg]])
                    ACC.free(pbk)
                    tt(S.dve, arch(16 + m), arch(16 + m), tmpf[eg][:], ALU.add, [Rar[16 + m], Rt[eg]], [Rar[16 + m]])
                    TMP.free(eg)
                WS.free(kpb)

            kws = [wload(t) for t in HT["Wout"]]
            w3s = [wslot[k][:].rearrange("p (k c) -> p k c", k=4) for k in kws]
            hn_js = []
            for s in range(4):
                bs = [ACC.get(), ACC.get()]
                for hf in range(2):
                    for kq in range(2):
                        k = kws[hf * 2 + kq]
                        for kc in range(4):
                            mm(bank(bs[hf]), arch(16 + kq * 4 + kc)[:, s * 128:(s + 1) * 128],
                               w3s[hf * 2 + kq][:, kc, :], kq == 0 and kc == 0, kq == 1 and kc == 3,
                               [Rw[k], Rar[16 + kq * 4 + kc]], [Rb[bs[hf]]])
                for hf in range(2):
                    hs = hbuf[:, s, hf * 512:(hf + 1) * 512]
                    tt(S.dve, hs, bank(bs[hf]), hs, ALU.add, [Rb[bs[hf]], Rh[s]], [Rh[s]])
                    ACC.free(bs[hf])
                si, rs = rstd_of(hbuf[:, s, :], [Rh[s]])
                j = TMP.get()
                tsmul(S.dve, tb(j)[:, 0:1024], hbuf[:, s, :], rs, [Rh[s], Rst[si]], [Rt[j]])
                STP.free(si)
                hn_js.append(j)
                if s >= 1:
                    transpose_to(tb(hn_js[s - 1]), [Rt[hn_js[s - 1]]], bufi, s - 1)
                    TMP.free(hn_js[s - 1])
            for k in kws:
                WS.free(k)
            transpose_to(tb(hn_js[3]), [Rt[hn_js[3]]], bufi, 3)
            TMP.free(hn_js[3])

            for tix, t in enumerate(HT["Wup"]):
                k = wload(t)
                w3 = wslot[k][:].rearrange("p (k c) -> p k c", k=8)
                for fl in range(2):
                    f = tix * 2 + fl
                    pool_ = ACC if f % 2 == 0 else ROT
                    zb = pool_.get()
                    for kc in range(8):
                        mm(bank(zb), w3[:, kc, fl * 128:(fl + 1) * 128], xnT[bufi][:, kc, :], kc == 0, kc == 7,
                           [Rw[k], RxnT[bufi]], [Rb[zb]])
                    r_ = TMP.get()
                    actf(tmpf[r_][:], bank(zb), AF.Relu, [Rb[zb]], [Rt[r_]])
                    pool_.free(zb)
                    tt(S.pool if f % 3 == 0 else S.dve, arch(f), tmpf[r_][:], tmpf[r_][:], ALU.mult,
                       [Rt[r_]], [Rar[f]])
                    TMP.free(r_)
                WS.free(k)

            if i + 2 < nt:
                carry = norm_stage(row_base + (i + 2) * T, on_pool=True)
            if i + 1 < nt:
                build_rope_tiles(i + 1)
                carry_q = wload(HT["qA"][0])
                q0gen = qA_gen(0, ROT, ROT, buf=(i + 1) % 2, out_ap=q0buf[:], Rout=[R("q0")], k=carry_q)
            else:
                q0gen = None

            for hf in range(2):
                bs = [ACC.get() for _ in range(4)]
                for kq in range(8):
                    k = wload(HT["Wdown"][hf * 8 + kq])
                    w3 = wslot[k][:].rearrange("p (k c) -> p k c", k=4)
                    for s in range(4):
                        for kc in range(4):
                            f = kq * 4 + kc
                            mm(bank(bs[s]), arch(f)[:, s * 128:(s + 1) * 128], w3[:, kc, :],
                               kq == 0 and kc == 0, kq == 7 and kc == 3, [Rw[k], Rar[f]], [Rb[bs[s]]])
                    WS.free(k)
                    if q0gen is not None and (hf * 8 + kq) >= 2 and (hf * 8 + kq) % 2 == 0:
                        try:
                            next(q0gen)
                        except StopIteration:
                            q0gen = None
                for s in range(4):
                    hs = hbuf[:, s, hf * 512:(hf + 1) * 512]
                    tt(S.dve, hs, bank(bs[s]), hs, ALU.add, [Rb[bs[s]], Rh[s]], [Rh[s]])
                    ACC.free(bs[s])

            if q0gen is not None:
                for _ in q0gen:
                    pass
                q0gen = None

            si, rss = rstd_batch([(hbuf[:, s, :], [Rh[s]]) for s in range(4)])
            for s in range(4):
                stt(S.dve, hbuf[:, s, :], hbuf[:, s, :], rss[s], gfb[:], ALU.mult, ALU.mult,
                    [Rh[s], Rst[si], R("gfb")], [Rh[s]])
                r0s = r0 + s * 128
                pdma(c_out[s], y_all[r0s:r0s + 128, :], hbuf[:, s, :], [Rh[s]], [])
            STP.free(si)

    row = 0
    for _ in cast_gen():
        pass
    for SL in seq_lens:
        process_sequence(row, SL, None)
        row += SL

    S.final_wait(S.sp, c_out)
    S.finalize_and_emit(st)
    st.close()
    return nc


def _stat8(W):
    return W.reshape(8, 128, 256).transpose(1, 0, 2).reshape(128, 2048)


def _k4(W):
    return W.reshape(4, 128, 512).transpose(1, 0, 2).reshape(128, 2048)


def _pack_weights(w_in, w_proj_a, w_proj_b, w_out, w_up, w_down):
    wt = np.empty((NH, 128, 2048), np.float32)
    perm = np.concatenate([np.r_[j * 64:(j + 1) * 64, (4 + j) * 64:(5 + j) * 64] for j in range(4)])
    qA = w_in[:, 0:512][:, perm]
    for h, t in enumerate(HT["qA"]):
        wt[t] = _stat8(qA[:, h * 256:(h + 1) * 256])
    wt[HT["kvA"][0]] = _stat8(w_in[:, 512:768])
    for h, t in enumerate(HT["qB"]):
        wt[t] = _stat8(w_in[:, 768 + h * 256:768 + (h + 1) * 256])
    for h, t in enumerate(HT["kB"]):
        wt[t] = _stat8(w_in[:, 1280 + h * 256:1280 + (h + 1) * 256])
    vB = w_in[:, 1792:2304]
    for q, t in enumerate(HT["vB"]):
        wt[t] = _k4(vB[q * 512:(q + 1) * 512, :])
    for h, t in enumerate(HT["gA"]):
        wt[t] = _stat8(w_in[:, 2304 + h * 256:2304 + (h + 1) * 256])
    for h, t in enumerate(HT["gB"]):
        wt[t] = _stat8(w_in[:, 3328 + h * 256:3328 + (h + 1) * 256])
    wpa = w_proj_a[perm, :]
    for h, t in enumerate(HT["WpA"]):
        wt[t] = _k4(wpa[:, h * 512:(h + 1) * 512])
    for h, t in enumerate(HT["WpB"]):
        wt[t] = _k4(w_proj_b[:, h * 512:(h + 1) * 512])
    for hf in range(2):
        for kq in range(2):
            wt[HT["Wout"][hf * 2 + kq]] = _k4(w_out[kq * 512:(kq + 1) * 512, hf * 512:(hf + 1) * 512])
    for h, t in enumerate(HT["Wup"]):
        wt[t] = _stat8(w_up[:, h * 256:(h + 1) * 256])
    for hf in range(2):
        for kq in range(8):
            wt[HT["Wdown"][hf * 8 + kq]] = _k4(w_down[kq * 512:(kq + 1) * 512, hf * 512:(hf + 1) * 512])
    return wt


def _constants():
    bf = ml_dtypes.bfloat16
    ident = np.eye(128, dtype=np.float32)
    bones = np.zeros((128, 128), np.float32)
    bones[0:64, 0:64] = 1.0 / 64
    bones[64:128, 64:128] = 1.0 / 64
    rotm = np.zeros((128, 128), np.float32)
    for i in range(64):
        rotm[2 * i + 1, 2 * i] = -1.0
        rotm[2 * i, 2 * i + 1] = 1.0
    cbf = np.concatenate([ident, bones, rotm], axis=1).astype(bf)
    inv = 10000.0 ** (-np.arange(0, 32, 2, dtype=np.float64) / 32.0)
    tabs = np.zeros((128, 4, 64), np.float64)
    pos = np.arange(64, dtype=np.float64)
    for p in range(128):
        i = (p % 64) // 2
        if i < 16:
            ang = pos * np.float64(np.float32(inv[i]))
            tabs[p, 0, :] = np.cos(ang)
            tabs[p, 1, :] = np.sin(ang)
        else:
            ang = pos * np.float64(np.float32(inv[i - 16]))
            tabs[p, 2, :] = np.cos(ang)
            tabs[p, 3, :] = np.sin(ang)
    tabs = tabs.astype(np.float32).reshape(128, 256)
    ind = np.zeros((32, 4096), np.float32)
    kc = np.arange(64)[:, None]
    qc = np.arange(64)[None, :]
    b = kc - qc + 15
    for bb in range(31):
        ind[bb] = (b == bb).astype(np.float32).reshape(-1)
    cstart = np.clip(qc - 8, 0, 48)
    cm = (kc >= cstart) & (kc < cstart + 16)
    ind[31] = np.where(cm, 0.0, -30000.0).astype(np.float32).reshape(-1)
    return cbf, tabs, ind


SEQ_LENS = [4096, 2048, 2048, 2048, 2048]
_CACHE = {}


def _shared_inputs(norm_mix_g, w_in, a_q_norm_g, a_k_norm_g, b_rel_pos_bias, w_proj_a, w_proj_b,
                   w_out, norm_mlp_g, w_mlp_up, w_mlp_down, norm_final_g):
    f = lambda a: np.ascontiguousarray(np.asarray(a, dtype=np.float32))
    wt = _pack_weights(f(w_in)[0], f(w_proj_a)[0], f(w_proj_b)[0], f(w_out)[0], f(w_mlp_up)[0],
                       f(w_mlp_down)[0])
    g1 = f(norm_mix_g)[0].reshape(8, 128).T
    g2 = f(norm_mlp_g)[0].reshape(8, 128).T
    gtabs = np.ascontiguousarray(np.concatenate([g1, g2], axis=1))
    smalls = np.ascontiguousarray(np.stack([np.tile(f(a_q_norm_g)[0], 2), np.tile(f(a_k_norm_g)[0], 2)], axis=1))
    gfb = np.ascontiguousarray(np.broadcast_to(f(norm_final_g)[None, :], (128, D)))
    rpb = f(b_rel_pos_bias)[0]
    rpbT = np.ones((32, 120), np.float32)
    rpbT[0:31] = rpb.reshape(120, 31).T
    cbf, tabs, ind = _constants()
    return {"wt": wt, "gtabs": gtabs, "smalls": smalls, "gfb": gfb, "rpbT": rpbT, "ind": ind,
            "cbf": cbf, "tabs": tabs}


def kernel(x_prompt, x_sample, norm_mix_g, w_in, a_q_norm_g, a_k_norm_g, b_rel_pos_bias,
           w_proj_a, w_proj_b, w_out, norm_mlp_g, w_mlp_up, w_mlp_down, norm_final_g):
    n = 8
    x_prompt = np.asarray(x_prompt, dtype=np.float32)
    x_sample = np.asarray(x_sample, dtype=np.float32)
    shared = _shared_inputs(norm_mix_g, w_in, a_q_norm_g, a_k_norm_g, b_rel_pos_bias, w_proj_a,
                            w_proj_b, w_out, norm_mlp_g, w_mlp_up, w_mlp_down, norm_final_g)
    nc = bass.Bass("TRN2", target_bir_lowering=False)
    build_program(nc, SEQ_LENS)
    in_maps = []
    for c in range(n):
        xa = np.concatenate([x_prompt[c].reshape(4096, D), x_sample[4 * c:4 * c + 4].reshape(8192, D)], axis=0)
        m = dict(shared)
        m["x_all"] = np.ascontiguousarray(xa)
        in_maps.append(m)
    res = run_bass_kernel_spmd(nc, in_maps, core_ids=list(range(n)))
    yp = np.empty((8, 4096, D), np.float32)
    ys = np.empty((32, 2048, D), np.float32)
    for c in range(n):
        y = res.results[c]["y_all"]
        yp[c] = y[0:4096]
        ys[4 * c:4 * c + 4] = y[4096:].reshape(4, 2048, D)
    return (yp, ys)
```

```python
import numpy as np
import ml_dtypes
from contextlib import ExitStack
import concourse.bass as bass
import concourse.mybir as mybir
from concourse.bass_utils import run_bass_kernel_spmd

F32 = mybir.dt.float32
BF16 = mybir.dt.bfloat16
ALU = mybir.AluOpType
AF = mybir.ActivationFunctionType

D = 1024
T = 512
EPS = 1e-6
QSCALE = 0.125
GRID_W = 64


class Res:
    __slots__ = ("name", "lw", "rd")

    def __init__(self, name):
        self.name = name
        self.lw = None
        self.rd = {}


class Lane:
    def __init__(self, name, is_dma=False):
        self.name = name
        self.is_dma = is_dma
        self.sem = None
        self.ops = []
        self.count = 0
        self.seen = {}


class Op:
    __slots__ = ("fn", "waits", "signal", "sigval", "chan")

    def __init__(self, fn):
        self.fn = fn
        self.waits = []
        self.signal = False
        self.sigval = 0
        self.chan = None


class _Noop:
    def then_inc(self, *a, **k):
        return self


class Sched:
    def __init__(self, nc):
        self.nc = nc
        self.pe = Lane("pe")
        self.act = Lane("act")
        self.dve = Lane("dve")
        self.pool = Lane("pool")
        self.sp = Lane("sp")
        self.engines = [self.pe, self.act, self.dve, self.pool, self.sp]
        self.chans = []

    def chan(self, name):
        c = Lane(name, is_dma=True)
        self.chans.append(c)
        return c

    def _deps(self, lane, reads, writes):
        deps = {}

        def add(l, i, same_ok):
            if l is lane and not same_ok:
                return
            if deps.get(l, -1) < i:
                deps[l] = i
        for r in reads:
            if r.lw is not None:
                add(r.lw[0], r.lw[1], True)
        for w in writes:
            if w.lw is not None:
                add(w.lw[0], w.lw[1], False)
            for l, i in w.rd.items():
                add(l, i, False)
        out = []
        for l, i in deps.items():
            if l.is_dma:
                i = l.count - 1
            if lane.seen.get(l, -1) >= i:
                continue
            lane.seen[l] = i
            out.append((l, i))
        return out

    def op(self, lane, fn, reads=(), writes=()):
        o = Op(fn)
        o.waits = self._deps(lane, reads, writes)
        idx = len(lane.ops)
        lane.ops.append(o)
        for r in reads:
            if r.rd.get(lane, -1) < idx:
                r.rd[lane] = idx
        for w in writes:
            w.lw = (lane, idx)
            w.rd = {}
        return o

    def dma(self, queue, chan, fn, reads=(), writes=()):
        o = Op(fn)
        o.chan = chan
        o.waits = self._deps(queue, reads, writes)
        queue.ops.append(o)
        idx = chan.count
        chan.count += 1
        for r in reads:
            if r.rd.get(chan, -1) < idx:
                r.rd[chan] = idx
        for w in writes:
            w.lw = (chan, idx)
            w.rd = {}
        return o

    def final_wait(self, lane, chans):
        o = Op(lambda e: _Noop())
        for c in chans:
            if c.count > 0:
                o.waits.append((c, c.count - 1))
        lane.ops.append(o)

    def finalize_and_emit(self, stack):
        nc = self.nc
        for l in self.engines + self.chans:
            l.sem = stack.enter_context(nc.semaphore("s_" + l.name))
        for l in self.engines:
            for o in l.ops:
                for (dl, di) in o.waits:
                    if not dl.is_dma:
                        dl.ops[di].signal = True
        for l in self.engines:
            v = 0
            for o in l.ops:
                if o.chan is None and o.signal:
                    v += 1
                o.sigval = v
        block = stack.enter_context(nc.Block())

        def emit(lane):
            def body(e):
                for o in lane.ops:
                    for (dl, di) in o.waits:
                        if dl.is_dma:
                            e.wait_ge(dl.sem, 16 * (di + 1))
                        else:
                            e.wait_ge(dl.sem, dl.ops[di].sigval)
                    ins = o.fn(e)
                    if o.chan is not None:
                        ins.then_inc(o.chan.sem, 16)
                    elif o.signal:
                        ins.then_inc(lane.sem, 1)
            return body
        block.tensor(emit(self.pe))
        block.scalar(emit(self.act))
        block.vector(emit(self.dve))
        block.gpsimd(emit(self.pool))
        block.sync(emit(self.sp))


class Pool_:
    def __init__(self, name, ids):
        self.name = name
        self.ids = list(ids)
        self.live = {i: False for i in self.ids}
        self.ptr = 0

    def get(self):
        n = len(self.ids)
        for k in range(n):
            i = self.ids[(self.ptr + k) % n]
            if not self.live[i]:
                self.live[i] = True
                self.ptr = (self.ptr + k + 1) % n
                return i
        raise RuntimeError("pool %s exhausted" % self.name)

    def get2(self):
        n = len(self.ids) // 2
        start = (self.ptr // 2)
        for k in range(n):
            j = (start + k) % n
            a, b = self.ids[2 * j], self.ids[2 * j + 1]
            if not self.live[a] and not self.live[b]:
                self.live[a] = self.live[b] = True
                self.ptr = (2 * j + 2) % len(self.ids)
                return a
        raise RuntimeError("pool %s exhausted (pair)" % self.name)

    def free(self, i, n=1):
        for k in range(n):
            assert self.live[i + k], (self.name, i + k)
            self.live[i + k] = False


HT = {}
_idx = 0
for _name, _n in [("qA", 2), ("kvA", 1), ("qB", 2), ("kB", 2), ("vB", 2), ("gA", 4), ("gB", 4),
                  ("WpA", 2), ("WpB", 2), ("Wout", 4), ("Wup", 16), ("Wdown", 16)]:
    HT[_name] = list(range(_idx, _idx + _n))
    _idx += _n
NH = _idx

SCALE_SPEC = {}
for _n in ["qA", "kvA", "qB", "kB", "gA", "gB"]:
    for _t in HT[_n]:
        SCALE_SPEC[_t] = (8, 0)
for _q, _t in enumerate(HT["vB"]):
    SCALE_SPEC[_t] = (4, 4 * _q)
for _t in HT["Wup"]:
    SCALE_SPEC[_t] = (8, 8)

RS = {0: (0, 0), 1: (0, 0), 2: (0, 1), 3: (2, 2), 4: (2, 2)}


def block_type(b, nblk):
    if b == 0:
        return 0, 0
    if b == 1:
        return 1, 0
    if b == nblk - 2:
        return 3, nblk - 5
    if b == nblk - 1:
        return 4, nblk - 5
    return 2, b - 2


def build_program(nc, seq_lens):
    NROWS = sum(seq_lens)
    S = Sched(nc)
    st = ExitStack()

    def din(name, shape, dt):
        return nc.dram_tensor(name, shape, dt, kind="ExternalInput").ap()

    x_all = din("x_all", [NROWS, D], F32)
    wt = din("wt", [NH, 128, 2048], F32)
    gtabs_d = din("gtabs", [128, 16], F32)
    smalls_d = din("smalls", [128, 2], F32)
    gf_d = din("gfb", [128, D], F32)
    rpbT_d = din("rpbT", [32, 120], F32)
    ind_d = din("ind", [32, 4096], F32)
    cbf_d = din("cbf", [128, 384], BF16)
    tabs_d = din("tabs", [128, 256], F32)
    y_all = nc.dram_tensor("y_all", [NROWS, D], F32, kind="ExternalOutput").ap()
    wsc = nc.dram_tensor("wsc", [NH, 128, 2048], BF16).ap()
    tz_d = nc.dram_tensor("tz_d", [120, 4096], F32).ap()
    e_d = nc.dram_tensor("e_d", [5, 128, 5120], BF16).ap()

    def sb(name, shape, dt):
        return st.enter_context(nc.sbuf_tensor(name, shape, dt))

    SMAX = max(max(seq_lens), 4096)
    NCH_MAX = SMAX // 128
    KA = sb("KA", [128, SMAX], BF16)
    VA = sb("VA", [128, NCH_MAX, 192], BF16)
    KBr = sb("KBr", [128, 4, 3, 512], BF16)
    VBr = sb("VBr", [128, 12, 4, 192], BF16)
    tabs = sb("tabss", [128, 4, 64], F32)
    Ct = sb("Ct", [128, 512], F32)
    St = sb("St", [128, 512], F32)
    Eint = sb("Eint", [128, 5120], BF16)
    Eedge = sb("Eedge", [128, 5120], BF16)
    NSLOT = 5
    wslot = [sb("wslot%d" % k, [128, 2048], BF16) for k in range(NSLOT)]
    wkvb = sb("wkvb", [128, 2048], BF16)
    hbuf = sb("hbuf", [128, 4, D], F32)
    xin = [sb("xin%d" % k, [128, D], F32) for k in range(2)]
    xnT = [sb("xnT%d" % k, [128, 8, 512], BF16) for k in range(2)]
    arena = sb("arena", [128, 32 * 512], BF16)
    pTb = [sb("pTb%d" % k, [128, 640], BF16) for k in range(3)]
    gfb = sb("gfbs", [128, D], F32)
    NTMP = 12
    tmpf = [sb("tmp%d" % k, [128, 512], F32) for k in range(NTMP)]
    cbf = sb("cbfs", [128, 384], BF16)
    gtabs = sb("gtabss", [128, 16], F32)
    smalls = sb("smallss", [128, 2], F32)
    stats = sb("stats", [128, 16 * 12], F32)
    rpbT = sb("rpbTs", [32, 120], F32)
    q0buf = sb("q0buf", [128, 512], BF16)
    ps = st.enter_context(nc.psum_tensor("ps", [128, 4096], F32))

    ident = cbf[:, 0:128]
    bones = cbf[:, 128:256]
    rotm = cbf[:, 256:384]

    RR = {}

    def R(name):
        r = RR.get(name)
        if r is None:
            r = RR[name] = Res(name)
        return r
    Rb = [R("bank%d" % b) for b in range(8)]
    Rw = [R("wslot%d" % k) for k in range(NSLOT)]
    Rt = [R("tmp%d" % k) for k in range(NTMP)]
    Rar = [R("ar%d" % k) for k in range(32)]
    Rh = [R("h%d" % s) for s in range(4)]
    Rxin = [R("xin%d" % k) for k in range(2)]
    RxnT = [R("xnT%d" % k) for k in range(2)]
    RKB = [R("KB%d" % k) for k in range(3)]
    RVB = [R("VB%d" % k) for k in range(3)]
    RpTb = [R("pTb%d" % k) for k in range(3)]
    Rst = [R("stat%d" % k) for k in range(16)]

    ACC = Pool_("acc", [0, 1, 2, 3])
    ROT = Pool_("rot", [4, 5, 6, 7])
    ACCA = Pool_("acca", [0, 1])
    FIL = Pool_("fil", [2, 3])
    TMP = Pool_("tmp", range(NTMP))
    WS = Pool_("ws", range(NSLOT))
    PTB = Pool_("ptb", range(3))
    PTA = Pool_("pta", range(3))
    STP = Pool_("stat", range(16))

    def bank(b, n=1):
        return ps[:, b * 512:(b + n) * 512]

    def tb(k):
        return tmpf[k][:].bitcast(BF16)

    def arch(k, n=1):
        return arena[:, k * 512:(k + n) * 512]

    c_set = S.chan("cset")
    c_wsc = [S.chan("cwsc%d" % k) for k in range(NSLOT)]
    c_tz = S.chan("ctz")
    c_tzk = S.chan("ctzk")
    c_ed = S.chan("ced")
    c_w = [S.chan("cw%d" % k) for k in range(NSLOT)]
    c_xin = [S.chan("cxin%d" % k) for k in range(2)]
    c_h = S.chan("chl")
    c_out = [S.chan("cout%d" % s) for s in range(4)]
    c_edge = S.chan("cedge")

    def mm(out, lhsT, rhs, start, stop, r, w):
        S.op(S.pe, lambda e: e.matmul(out, lhsT=lhsT, rhs=rhs, start=start, stop=stop), r, w)

    def actf(out, in_, func, r, w, scale=1.0, bias=0.0, accum=None):
        if accum is None:
            S.op(S.act, lambda e: e.activation(out=out, in_=in_, func=func, scale=scale, bias=bias), r, w)
        else:
            S.op(S.act, lambda e: e.activation(out=out, in_=in_, func=func, scale=scale, bias=bias,
                                               accum_out=accum), r, w)

    def copy_on(lane, out, in_, r, w):
        if lane is S.act:
            S.op(S.act, lambda e: e.activation(out=out, in_=in_, func=AF.Copy), r, w)
        else:
            S.op(lane, lambda e: e.tensor_copy(out=out, in_=in_), r, w)

    def tt(lane, out, in0, in1, op, r, w):
        S.op(lane, lambda e: e.tensor_tensor(out=out, in0=in0, in1=in1, op=op), r, w)

    def tsmul(lane, out, in0, scalar, r, w):
        S.op(lane, lambda e: e.tensor_scalar_mul(out=out, in0=in0, scalar1=scalar), r, w)

    def tsadd(lane, out, in0, scalar, r, w):
        S.op(lane, lambda e: e.tensor_scalar_add(out=out, in0=in0, scalar1=scalar), r, w)

    def stt(lane, out, in0, scalar, in1, op0, op1, r, w):
        S.op(lane, lambda e: e.scalar_tensor_tensor(out=out, in0=in0, scalar=scalar, in1=in1,
                                                    op0=op0, op1=op1), r, w)

    def recip_ln(out, in_, r, w, bias=0.0):
        actf(out, in_, AF.Ln, r, w, bias=bias)

    def recip_exp(buf, rw):
        actf(buf, buf, AF.Exp, rw, rw, scale=-1.0)

    def memset(lane, ap, val, w):
        S.op(lane, lambda e: e.memset(ap, val), (), w)

    def sdma(chan, out, in_, r, w):
        S.dma(S.sp, chan, lambda e: e.dma_start(out=out, in_=in_), r, w)

    def pdma(chan, out, in_, r, w):
        S.dma(S.pool, chan, lambda e: e.dma_start(out=out, in_=in_), r, w)

    rr_state = {"evac": 0}

    def evac_lane():
        rr_state["evac"] += 1
        return S.act if rr_state["evac"] % 3 == 0 else S.dve

    sdma(c_set, cbf[:], cbf_d, (), [R("cbf")])
    sdma(c_set, gtabs[:], gtabs_d, (), [R("gtabs")])
    sdma(c_set, smalls[:], smalls_d, (), [R("smalls")])
    sdma(c_set, gfb[:], gf_d, (), [R("gfb")])
    sdma(c_set, tabs[:].rearrange("p a b -> p (a b)"), tabs_d, (), [R("tabs")])
    sdma(c_set, rpbT[:], rpbT_d, (), [R("rpbT")])
    hst = hbuf[:].rearrange("p s d -> p (s d)")
    arena_f = arena[:].bitcast(F32)
    stage_f = [(hst[:, 0:2048], [Rh[0], Rh[1]]), (hst[:, 2048:4096], [Rh[2], Rh[3]])]
    for q in range(4):
        stage_f.append((arena_f[:, q * 2048:(q + 1) * 2048], Rar[8 * q:8 * q + 8]))
    c_wst = [S.chan("cwst%d" % k) for k in range(len(stage_f))]
    cast_state = {"n": 0}

    def cast_load(t):
        fs = cast_state["n"] % len(stage_f)
        cast_state["n"] += 1
        src, Rsrc = stage_f[fs]
        sdma(c_wst[fs], src, wt[t], (), Rsrc)
        return fs

    def cast_one(t, fs=None):
        if fs is None:
            fs = cast_load(t)
        src, Rsrc = stage_f[fs]
        is_kv = (t == HT["kvA"][0])
        if is_kv:
            dst, Rdst, k = wkvb[:], [R("wkvb")], None
        else:
            k = WS.get()
            dst, Rdst = wslot[k][:], [Rw[k]]
        spec = SCALE_SPEC.get(t)
        lane = S.act if t % 2 == 0 else S.dve
        if spec is None:
            copy_on(lane, dst, src, Rsrc, Rdst)
        else:
            nkc, c0 = spec
            wdt = 2048 // nkc
            for kc in range(nkc):
                o_ = dst[:, kc * wdt:(kc + 1) * wdt]
                i_ = src[:, kc * wdt:(kc + 1) * wdt]
                sc = gtabs[:, c0 + kc:c0 + kc + 1]
                if lane is S.act:
                    actf(o_, i_, AF.Copy, Rsrc + [R("gtabs")], Rdst, scale=sc)
                else:
                    tsmul(S.dve, o_, i_, sc, Rsrc + [R("gtabs")], Rdst)
        if not is_kv:
            pdma(c_wsc[k], wsc[t], dst, Rdst, [R("wsc%d" % t)])
            WS.free(k)

    def cast_gen():
        ts = [t for t in range(NH) if t != HT["kvA"][0]]
        PF = len(stage_f) - 1
        pend = []
        for q, t in enumerate(ts):
            while len(pend) < PF and len(pend) + q < len(ts):
                t2 = ts[q + len(pend)]
                pend.append((t2, cast_load(t2)))
            t1, fs = pend.pop(0)
            assert t1 == t
            cast_one(t, fs)
            yield

    cast_one(HT["kvA"][0])

    ind_sb = arena_f[0:32, 0:4096]
    tz_sb = arena_f[0:120, 4096:8192]
    sdma(c_set, ind_sb, ind_d, (), Rar)
    for n in range(8):
        mm(bank(n)[0:120, :], rpbT[:, :], ind_sb[:, n * 512:(n + 1) * 512], True, True,
           Rar + [R("rpbT")], [Rb[n]])
        copy_on(S.dve if n % 2 else S.act, tz_sb[:, n * 512:(n + 1) * 512], bank(n)[0:120, :], [Rb[n]], Rar)
    sdma(c_tz, tz_d, tz_sb, Rar, [R("tz_d")])
    VBflat = VBr[:].rearrange("p a b c -> p (a b c)")
    ez = [VBflat[:, 0:3840], VBflat[:, 4608:8448]]
    for grp in range(2):
        for kr2 in range(2):
            sdma(c_tzk,
                 hst[kr2 * 64:(kr2 + 1) * 64, 0:3840].rearrange("p (a q) -> p a q", q=64),
                 tz_d[grp * 60:(grp + 1) * 60, :].rearrange("a (k q) -> k a q", q=64),
                 [R("tz_d")], Rh)
        actf(ez[grp], hst[:, 0:3840], AF.Copy, Rh, RVB, scale=1.0 / QSCALE)

    def build_E(ty, dst, Rd, lane):
        dv = dst.rearrange("p (h c q) -> p h c q", h=8, c=5)
        for ch in range(5):
            for kr2 in range(2):
                for qr2 in range(2):
                    krow = 2 * ch + kr2
                    rs = RS[ty][qr2]
                    if not (rs <= krow < rs + 8):
                        continue
                    a_ = krow - (2 * ty + qr2) + 7
                    assert 0 <= a_ <= 14
                    for grp in range(2):
                        i_ = ez[grp][kr2 * 64:(kr2 + 1) * 64, :].rearrange(
                            "p (h a q) -> p h a q", h=4, a=15)[:, :, a_, :]
                        o_ = dv[kr2 * 64:(kr2 + 1) * 64, grp * 4:(grp + 1) * 4, ch, qr2 * 64:(qr2 + 1) * 64]
                        copy_on(lane, o_, i_, RVB, Rd)

    VAflat = VA[:].rearrange("p a b -> p (a b)")
    KBflat = KBr[:].rearrange("p a b c -> p (a b c)")
    stagings = {0: (VAflat[:, 0:5120], [R("VA")], S.dve), 1: (KBflat[:, 0:5120], RKB, S.act),
                3: (Eedge[:], [R("Eedge")], S.act), 4: (arena[:, 0:5120], Rar[0:10], S.pool)}
    stagings[2] = (Eint[:], [R("Eint")], S.dve)
    for n_, ty in enumerate((2, 0, 1, 3, 4)):
        dst, Rd, lane = stagings[ty]
        memset(S.pool if n_ % 2 else S.dve, dst, -240000.0, Rd)
    for ty in (2, 1, 4, 0, 3):
        dst, Rd, lane = stagings[ty]
        build_E(ty, dst, Rd, lane)
        if ty != 2:
            sdma(c_ed, e_d[ty], dst, Rd, [R("e_d")])
    memset(S.pool, VA[:, :, 64:128], 1.0, [R("VA")])
    memset(S.pool, VBr[:, :, :, 64:128], 1.0, RVB)

    def wload(t, on_pool=False):
        k = WS.get()
        (pdma if on_pool else sdma)(c_w[k], wslot[k][:], wsc[t], [R("wsc%d" % t)], [Rw[k]])
        return k

    def rstd_batch(srcs):
        si = STP.get()
        n = len(srcs)
        b0 = 12 * si
        j = TMP.get()
        for q, (src_ap, Rsrc) in enumerate(srcs):
            actf(tb(j)[:, 0:1024], src_ap, AF.Square, Rsrc, [Rt[j], Rst[si]], accum=stats[:, b0 + q:b0 + q + 1])
        TMP.free(j)
        actf(stats[:, b0 + 4:b0 + 4 + n], stats[:, b0:b0 + n], AF.Ln, [Rst[si]], [Rst[si]], scale=1.0 / D, bias=EPS)
        actf(stats[:, b0 + 8:b0 + 8 + n], stats[:, b0 + 4:b0 + 4 + n], AF.Exp, [Rst[si]], [Rst[si]], scale=-0.5)
        return si, [stats[:, b0 + 8 + q:b0 + 9 + q] for q in range(n)]

    def rstd_of(src_ap, Rsrc):
        si, cols = rstd_batch([(src_ap, Rsrc)])
        return si, cols[0]

    def transpose_to(src_bf, Rsrc, dst_buf, s):
        b = ROT.get2()
        for kc in range(8):
            mm(bank(b, 2)[:, kc * 128:(kc + 1) * 128], src_bf[:, kc * 128:(kc + 1) * 128], ident,
               True, True, Rsrc + [R("cbf")], [Rb[b], Rb[b + 1]])
        copy_on(evac_lane(), xnT[dst_buf][:, :, s * 128:(s + 1) * 128],
                bank(b, 2).rearrange("p (k t) -> p k t", k=8), [Rb[b], Rb[b + 1]], [RxnT[dst_buf]])
        ROT.free(b, 2)

    xin_rr = {"k": 0}

    def norm_stage(row0, on_pool=False):
        js = []
        for s in range(4):
            k = xin_rr["k"]
            xin_rr["k"] = 1 - k
            r0 = row0 + s * 128
            (pdma if on_pool else sdma)(c_xin[k], xin[k][:], x_all[r0:r0 + 128, :], (), [Rxin[k]])
            si, rs = rstd_of(xin[k][:], [Rxin[k]])
            j = TMP.get()
            tsmul(S.dve, tb(j)[:, 0:1024], xin[k][:], rs, [Rxin[k], Rst[si]], [Rt[j]])
            STP.free(si)
            js.append(j)
        return js

    def transpose_stage(js, dst_buf):
        for s, j in enumerate(js):
            transpose_to(tb(j), [Rt[j]], dst_buf, s)
            TMP.free(j)

    def make_xnT(row0, dst_buf):
        transpose_stage(norm_stage(row0), dst_buf)

    def build_rope_tiles(tile):
        row0 = tile * 8
        for (dst, Rd, a) in ((Ct, R("Ct"), 0), (St, R("St"), 1)):
            S.op(S.pool, (lambda dst, a: lambda e: e.tensor_tensor(
                out=dst[:].rearrange("p (r c) -> p r c", c=64),
                in0=tabs[:, a, row0:row0 + 8].unsqueeze(2).to_broadcast([128, 8, 64]),
                in1=tabs[:, 2 + a, :].unsqueeze(1).to_broadcast([128, 8, 64]),
                op=ALU.add))(dst, a), [R("tabs")], [Rd])

    def headnorm_rope(zb, zpool, gcol, out_ap, Rout, spool=None):
        spool = spool or ROT
        sq = TMP.get()
        actf(tb(sq)[:, 0:512], bank(zb), AF.Square, [Rb[zb]], [Rt[sq]])
        yield
        sbk = spool.get()
        mm(bank(sbk), bones, tb(sq)[:, 0:512], True, True, [Rt[sq], R("cbf")], [Rb[sbk]])
        TMP.free(sq)
        ln = TMP.get()
        actf(tmpf[ln][:], bank(sbk), AF.Ln, [Rb[sbk]], [Rt[ln]], bias=EPS)
        spool.free(sbk)
        actf(tmpf[ln][:], tmpf[ln][:], AF.Exp, [Rt[ln]], [Rt[ln]], scale=-0.5)
        kn = TMP.get()
        stt(S.dve, tb(kn)[:, 0:512], bank(zb), gcol, tmpf[ln][:], ALU.mult, ALU.mult,
            [Rb[zb], Rt[ln], R("smalls")], [Rt[kn]])
        zpool.free(zb)
        TMP.free(ln)
        yield
        rb = spool.get()
        mm(bank(rb), rotm, tb(kn)[:, 0:512], True, True, [Rt[kn], R("cbf")], [Rb[rb]])
        t1 = TMP.get()
        ew = S.dve if spool is FIL else S.pool
        tt(ew, tmpf[t1][:], tb(kn)[:, 0:512], Ct[:], ALU.mult, [Rt[kn], R("Ct")], [Rt[t1]])
        TMP.free(kn)
        t2 = TMP.get()
        tt(S.dve, tmpf[t2][:], bank(rb), St[:], ALU.mult, [Rb[rb], R("St")], [Rt[t2]])
        spool.free(rb)
        tt(ew, out_ap, tmpf[t1][:], tmpf[t2][:], ALU.add, [Rt[t1], Rt[t2]], Rout)
        TMP.free(t1)
        TMP.free(t2)
        yield

    def run_jobs(jobs, lag=1):
        active = []
        jobs = list(jobs)
        while jobs or active:
            if jobs:
                active.append(jobs.pop(0))
            nxt = []
            for g in active:
                try:
                    next(g)
                    nxt.append(g)
                except StopIteration:
                    pass
            active = nxt

    def process_sequence(row_base, SL, extra_gen=None):
        nt = SL // T
        nch = SL // 128
        nblk = nch
        wkv3 = wkvb[:].rearrange("p (k c) -> p k c", k=8)

        def p1_job(tile):
            buf = tile % 2
            js = norm_stage(row_base + tile * T)
            yield
            transpose_stage(js, buf)
            yield
            zb = ACC.get()
            for kc in range(8):
                mm(bank(zb), wkv3[:, kc, 0:128], xnT[buf][:, kc, :], kc == 0, kc == 7,
                   [R("wkvb"), RxnT[buf]], [Rb[zb]])
            for s in range(4):
                vb_ = ROT.get()
                for kc in range(8):
                    mm(bank(vb_)[:, 0:128], xnT[buf][:, kc, s * 128:(s + 1) * 128], wkv3[:, kc, 128:256],
                       kc == 0, kc == 7, [R("wkvb"), RxnT[buf]], [Rb[vb_]])
                c = tile * 4 + s
                copy_on(evac_lane(), VA[:, c, :].rearrange("p (a d) -> p a d", a=3)[:, 0:3:2, :],
                        bank(vb_)[:, 0:128].rearrange("p (a d) -> p a d", a=2), [Rb[vb_]], [R("VA")])
                ROT.free(vb_)
            hn = headnorm_rope(zb, ACC, smalls[:, 1:2], KA[:, tile * T:(tile + 1) * T], [R("KA")])
            next(hn)
            yield
            next(hn)
            yield
            build_rope_tiles(tile)
            for _ in hn:
                pass

        jobs = [p1_job(t) for t in range(nt)]
        active = []
        while jobs or active:
            if jobs:
                active.append(jobs.pop(0))
            nxt_ = []
            for g in active:
                try:
                    next(g)
                    nxt_.append(g)
                except StopIteration:
                    pass
            active = nxt_
            if extra_gen is not None:
                for _ in range(6):
                    try:
                        next(extra_gen)
                    except StopIteration:
                        extra_gen = None
                        break
        if extra_gen is not None:
            for _ in extra_gen:
                pass

        def proj_kvB(tile):
            buf = tile % 2
            slot = tile % 3
            for half, t in enumerate(HT["kB"]):
                k = wload(t)
                w3 = wslot[k][:].rearrange("p (k c) -> p k c", k=8)
                for jl in range(2):
                    jj = half * 2 + jl
                    zb = ROT.get()
                    for kc in range(8):
                        mm(bank(zb), w3[:, kc, jl * 128:(jl + 1) * 128], xnT[buf][:, kc, :], kc == 0, kc == 7,
                           [Rw[k], RxnT[buf]], [Rb[zb]])
                    copy_on(evac_lane(), KBr[:, jj, slot, :], bank(zb), [Rb[zb]], [RKB[slot]])
                    ROT.free(zb)
                WS.free(k)
            bs = [ACC.get() for _ in range(4)]
            for kq, t in enumerate(HT["vB"]):
                k = wload(t)
                w3 = wslot[k][:].rearrange("p (k c) -> p k c", k=4)
                for s in range(4):
                    for kc in range(4):
                        mm(bank(bs[s]), xnT[buf][:, kq * 4 + kc, s * 128:(s + 1) * 128], w3[:, kc, :],
                           kq == 0 and kc == 0, kq == 1 and kc == 3, [Rw[k], RxnT[buf]], [Rb[bs[s]]])
                WS.free(k)
            for s in range(4):
                c = slot * 4 + s
                copy_on(evac_lane(),
                        VBr[:, c, :, :].rearrange("p j (a d) -> p j a d", a=3)[:, :, 0:3:2, :],
                        bank(bs[s]).rearrange("p (j a d) -> p j a d", j=4, a=2), [Rb[bs[s]]], [RVB[slot]])
                ACC.free(bs[s])

        make_xnT(row_base, 0)
        proj_kvB(0)
        carry = norm_stage(row_base + T) if nt > 1 else None
        carry_q = None
        for i in range(nt):
            bufi = i % 2
            nxt_js = carry
            carry = None

            wq = {}

            def qw(name, idx, nuse):
                key = (name, idx)
                if key not in wq:
                    wq[key] = [wload(HT[name][idx]), nuse]
                ent = wq[key]
                return ent[0]

            def qw_done(name, idx):
                ent = wq[(name, idx)]
                ent[1] -= 1
                if ent[1] == 0:
                    WS.free(ent[0])

            def qA_gen(j, zpool, spool, buf=None, out_ap=None, Rout=None, k=None):
                buf = bufi if buf is None else buf
                own = k is None
                if own:
                    k = qw("qA", j // 2, 2)
                w3 = wslot[k][:].rearrange("p (k c) -> p k c", k=8)
                jl = j % 2
                zb = zpool.get()
                for kc in range(8):
                    mm(bank(zb), w3[:, kc, jl * 128:(jl + 1) * 128], xnT[buf][:, kc, :], kc == 0, kc == 7,
                       [Rw[k], RxnT[buf]], [Rb[zb]])
                    if kc in (1, 3, 5):
                        yield
                if own:
                    qw_done("qA", j // 2)
                yield from headnorm_rope(zb, zpool, smalls[:, 0:1], arch(j) if out_ap is None else out_ap,
                                         [Rar[j]] if Rout is None else Rout, spool)

            def qB_gen(jj):
                k = qw("qB", jj // 2, 2)
                w3 = wslot[k][:].rearrange("p (k c) -> p k c", k=8)
                jl = jj % 2
                zb = FIL.get()
                for kc in range(8):
                    mm(bank(zb), w3[:, kc, jl * 128:(jl + 1) * 128], xnT[bufi][:, kc, :], kc == 0, kc == 7,
                       [Rw[k], RxnT[bufi]], [Rb[zb]])
                    if kc in (1, 3, 5):
                        yield
                qw_done("qB", jj // 2)
                copy_on(S.dve, arch(4 + jj), bank(zb), [Rb[zb]], [Rar[4 + jj]])
                FIL.free(zb)
                yield

            def T_gen(s_, j_, dst_buf):
                for hh in range(2):
                    b_ = FIL.get()
                    for kq_ in range(4):
                        kc = hh * 4 + kq_
                        mm(bank(b_)[:, kq_ * 128:(kq_ + 1) * 128], tb(j_)[:, kc * 128:(kc + 1) * 128], ident,
                           True, True, [Rt[j_], R("cbf")], [Rb[b_]])
                    if hh == 1:
                        TMP.free(j_)
                    copy_on(S.dve, xnT[dst_buf][:, hh * 4:(hh + 1) * 4, s_ * 128:(s_ + 1) * 128],
                            bank(b_).rearrange("p (k t) -> p k t", k=4), [Rb[b_]], [RxnT[dst_buf]])
                    FIL.free(b_)
                    yield

            def kB_gen(jj, tile):
                buf = tile % 2
                k = qw("kB", jj // 2, 2)
                w3 = wslot[k][:].rearrange("p (k c) -> p k c", k=8)
                jl = jj % 2
                zb = FIL.get()
                for kc in range(8):
                    mm(bank(zb), w3[:, kc, jl * 128:(jl + 1) * 128], xnT[buf][:, kc, :], kc == 0, kc == 7,
                       [Rw[k], RxnT[buf]], [Rb[zb]])
                    if kc in (1, 3, 5):
                        yield
                qw_done("kB", jj // 2)
                copy_on(S.dve, KBr[:, jj, tile % 3, :], bank(zb), [Rb[zb]], [RKB[tile % 3]])
                FIL.free(zb)
                yield

            def vB_gen(s_, tile):
                buf = tile % 2
                slot = tile % 3
                zb = FIL.get()
                for kq in range(2):
                    k = qw("vB", kq, 4)
                    w3 = wslot[k][:].rearrange("p (k c) -> p k c", k=4)
                    for kc in range(4):
                        mm(bank(zb), xnT[buf][:, kq * 4 + kc, s_ * 128:(s_ + 1) * 128], w3[:, kc, :],
                           kq == 0 and kc == 0, kq == 1 and kc == 3, [Rw[k], RxnT[buf]], [Rb[zb]])
                        if kc == 1:
                            yield
                    qw_done("vB", kq)
                    if kq == 0:
                        yield
                c_ = slot * 4 + s_
                copy_on(S.dve,
                        VBr[:, c_, :, :].rearrange("p j (a d) -> p j a d", a=3)[:, :, 0:3:2, :],
                        bank(zb).rearrange("p (j a d) -> p j a d", j=4, a=2), [Rb[zb]], [RVB[slot]])
                FIL.free(zb)
                yield

            def gA_gen(m):
                k = qw("gA", m // 2, 2)
                w3 = wslot[k][:].rearrange("p (k c) -> p k c", k=8)
                ml = m % 2
                zb = FIL.get()
                for kc in range(8):
                    mm(bank(zb), w3[:, kc, ml * 128:(ml + 1) * 128], xnT[bufi][:, kc, :], kc == 0, kc == 7,
                       [Rw[k], RxnT[bufi]], [Rb[zb]])
                    if kc in (1, 3, 5):
                        yield
                qw_done("gA", m // 2)
                copy_on(S.dve, arch(16 + m), bank(zb), [Rb[zb]], [Rar[16 + m]])
                FIL.free(zb)
                yield

            if i == 0:
                build_rope_tiles(0)
                for _ in qA_gen(0, ACC, ROT, out_ap=q0buf[:], Rout=[R("q0")]):
                    pass
            else:
                wq[("qA", 0)] = [carry_q, 1]
                carry_q = None
            fillers = []
            for j in (1, 2, 3):
                fillers.append((j, qA_gen(j, FIL, FIL)))
            for jj in range(4):
                fillers.append((99, qB_gen(jj)))
            if nxt_js is not None:
                for s_ in range(4):
                    fillers.append((99, T_gen(s_, nxt_js[s_], (i + 1) % 2)))
                for jj in range(4):
                    fillers.append((99, kB_gen(jj, i + 1)))
                for s_ in range(4):
                    fillers.append((99, vB_gen(s_, i + 1)))
            for m in range(8):
                fillers.append((99, gA_gen(m)))
            n_units_est = 3 * 7 + 4 * 5 + (8 + 20 + 20 if nxt_js is not None else 0) + 40
            fstate = {"done": 0}

            def pump(n=1):
                while n > 0 and fillers:
                    try:
                        next(fillers[0][1])
                        fstate["done"] += 1
                        n -= 1
                    except StopIteration:
                        fillers.pop(0)

            def drain(deadline):
                while fillers and fillers[0][0] <= deadline:
                    try:
                        next(fillers[0][1])
                        fstate["done"] += 1
                    except StopIteration:
                        fillers.pop(0)

            r0 = row_base + i * T
            pdma(c_h, hbuf[:], x_all[r0:r0 + T, :].rearrange("(s p) d -> p s d", p=128), (), Rh)

            tot_steps = 4 * nch
            for j in range(4):
                qsrc = q0buf[:] if j == 0 else arch(j)
                Rq = R("q0") if j == 0 else Rar[j]
                drain(j)
                accA = ACCA.get()
                accB = ACCA.get()
                prev = None
                for c in range(nch + 1):
                    cur = None
                    step = j * nch + c
                    while fillers and fstate["done"] < (step + 1) * n_units_est / tot_steps:
                        pump(1)
                    if j >= 1 and c == 1:
                        pump(4)
                    if c < nch:
                        r2 = ROT.get2()
                        mm(bank(r2), KA[0:64, c * 128:(c + 1) * 128], qsrc[0:64, :], True, True,
                           [R("KA"), Rq], [Rb[r2]])
                        mm(bank(r2 + 1), KA[64:128, c * 128:(c + 1) * 128], qsrc[64:128, :], True, True,
                           [R("KA"), Rq], [Rb[r2 + 1]])
                        pt = PTA.get()
                        actf(arch(24 + 2 * pt, 2), bank(r2, 2), AF.Exp, [Rb[r2], Rb[r2 + 1]],
                             [Rar[24 + 2 * pt], Rar[25 + 2 * pt]], scale=QSCALE)
                        ROT.free(r2, 2)
                        cur = (c, pt)
                    if prev is not None:
                        pc, ppt = prev
                        mm(bank(accA), VA[:, pc, 0:128], arch(24 + 2 * ppt), pc == 0, pc == nch - 1,
                           [R("VA"), Rar[24 + 2 * ppt]], [Rb[accA]])
                        mm(bank(accB), VA[:, pc, 64:192], arch(25 + 2 * ppt), pc == 0, pc == nch - 1,
                           [R("VA"), Rar[25 + 2 * ppt]], [Rb[accB]])
                        PTA.free(ppt)
                    prev = cur
                rc = TMP.get()
                rc2 = TMP.get()
                recip_ln(tmpf[rc][64:128, :], bank(accA)[64:128, :], [Rb[accA]], [Rt[rc]])
                recip_ln(tmpf[rc2][0:64, :], bank(accB)[0:64, :], [Rb[accB]], [Rt[rc2]])
                recip_exp(tmpf[rc][64:128, :], [Rt[rc]])
                recip_exp(tmpf[rc2][0:64, :], [Rt[rc2]])
                tt(S.dve, arch(8 + j)[0:64, :], bank(accA)[0:64, :], tmpf[rc][64:128, :], ALU.mult,
                   [Rb[accA], Rt[rc]], [Rar[8 + j]])
                tt(S.dve, arch(8 + j)[64:128, :], bank(accB)[64:128, :], tmpf[rc2][0:64, :], ALU.mult,
                   [Rb[accB], Rt[rc2]], [Rar[8 + j]])
                TMP.free(rc)
                TMP.free(rc2)
                ACCA.free(accA)
                ACCA.free(accB)
            drain(99)

            GBS = [0, 1, 2, 3, 24, 25, 26, 27]

            def tA_gen(m):
                k = qw("WpA", m // 4, 4)
                wp = wslot[k][:].rearrange("p (k c) -> p k c", k=4)
                ml = m % 4
                pbk = FIL.get()
                for kc in range(4):
                    mm(bank(pbk), wp[:, kc, ml * 128:(ml + 1) * 128], arch(8 + kc), kc == 0, kc == 3,
                       [Rw[k], Rar[8 + kc]], [Rb[pbk]])
                    if kc == 1:
                        yield
                qw_done("WpA", m // 4)
                eg = TMP.get()
                actf(tmpf[eg][:], arch(16 + m), AF.Exp, [Rar[16 + m]], [Rt[eg]], scale=-1.0)
                actf(tmpf[eg][:], tmpf[eg][:], AF.Ln, [Rt[eg]], [Rt[eg]], bias=1.0)
                yield
                actf(tmpf[eg][:], tmpf[eg][:], AF.Exp, [Rt[eg]], [Rt[eg]], scale=-1.0)
                tt(S.dve, arch(16 + m), bank(pbk), tmpf[eg][:], ALU.mult, [Rb[pbk], Rt[eg]], [Rar[16 + m]])
                FIL.free(pbk)
                TMP.free(eg)
                yield

            def gB_gen(m):
                k = qw("gB", m // 2, 2)
                w3 = wslot[k][:].rearrange("p (k c) -> p k c", k=8)
                ml = m % 2
                zb = FIL.get()
                for kc in range(8):
                    mm(bank(zb), w3[:, kc, ml * 128:(ml + 1) * 128], xnT[bufi][:, kc, :], kc == 0, kc == 7,
                       [Rw[k], RxnT[bufi]], [Rb[zb]])
                    if kc in (1, 3, 5):
                        yield
                qw_done("gB", m // 2)
                copy_on(S.dve, arch(GBS[m]), bank(zb), [Rb[zb]], [Rar[GBS[m]]])
                FIL.free(zb)
                yield

            for m in range(8):
                fillers.append((99, tA_gen(m)))
            for m in range(8):
                fillers.append((99, gB_gen(m)))
            n_units_b = 8 * 4 + 8 * 5
            fstate["done"] = 0

            for blk in range(4):
                b = i * 4 + blk
                ty, cs0 = block_type(b, nblk)
                if ty == 2:
                    E, RE = Eint, R("Eint")
                else:
                    sdma(c_edge, Eedge[:], e_d[ty], [R("e_d")], [R("Eedge")])
                    E, RE = Eedge, R("Eedge")
                Ev = E[:].rearrange("p (h f) -> p h f", h=8)
                bx = [ACCA.get(), ACCA.get()]
                prev = None
                for h in range(9):
                    cur = None
                    while fillers and fstate["done"] < (blk * 9 + h + 1) * n_units_b / 36.0:
                        pump(1)
                    if h < 8:
                        jj, hp = h // 2, h % 2
                        r2 = ROT.get2()
                        mm(bank(r2, 2)[:, 0:512], ident, Ev[:, h, 0:512], True, False,
                           [RE, R("cbf")], [Rb[r2], Rb[r2 + 1]])
                        mm(bank(r2, 2)[:, 512:640], ident, Ev[:, h, 512:640], True, False,
                           [RE, R("cbf")], [Rb[r2], Rb[r2 + 1]])
                        for ch in range(5):
                            cs = cs0 + ch
                            tt_, s_ = cs // 4, cs % 4
                            mm(bank(r2, 2)[:, ch * 128:(ch + 1) * 128],
                               KBr[hp * 64:(hp + 1) * 64, jj, tt_ % 3, s_ * 128:(s_ + 1) * 128],
                               arch(4 + jj)[hp * 64:(hp + 1) * 64, blk * 128:(blk + 1) * 128],
                               False, ch in (3, 4),
                               [RKB[tt_ % 3], Rar[4 + jj]], [Rb[r2], Rb[r2 + 1]])

                        pt = PTB.get()
                        actf(pTb[pt][:], bank(r2, 2)[:, 0:640], AF.Exp, [Rb[r2], Rb[r2 + 1]], [RpTb[pt]],
                             scale=QSCALE)
                        ROT.free(r2, 2)
                        cur = (h, pt)
                    if prev is not None:
                        ph, ppt = prev
                        jj, hp = ph // 2, ph % 2
                        ob = bx[ph // 4]
                        for ch in range(5):
                            cs = cs0 + ch
                            tt_, s_ = cs // 4, cs % 4
                            mm(bank(ob)[:, (ph % 4) * 128:(ph % 4 + 1) * 128],
                               VBr[:, (tt_ % 3) * 4 + s_, jj, hp * 64:hp * 64 + 128],
                               pTb[ppt][:, ch * 128:(ch + 1) * 128], ch == 0, ch == 4,
                               [RVB[tt_ % 3], RpTb[ppt]], [Rb[ob]])
                        PTB.free(ppt)
                    prev = cur
                rcs = [TMP.get(), TMP.get()]
                for half in range(2):
                    recip_ln(tmpf[rcs[half]][:], bank(bx[half]), [Rb[bx[half]]], [Rt[rcs[half]]])
                for half in range(2):
                    recip_exp(tmpf[rcs[half]][:], [Rt[rcs[half]]])
                for half in range(2):
                    ob = bx[half]
                    rc = rcs[half]
                    ov = bank(ob).rearrange("p (h q) -> p h q", h=4)
                    rv = tmpf[rc][:].rearrange("p (h q) -> p h q", h=4)
                    dst = arena[:, (12 + 2 * half) * 512:(14 + 2 * half) * 512].rearrange(
                        "p (j t) -> p j t", j=2)[:, :, blk * 128:(blk + 1) * 128]
                    Rd = [Rar[12 + 2 * half], Rar[13 + 2 * half]]
                    tt(S.dve, dst[0:64], ov[0:64, 0:4:2, :], rv[64:128, 0:4:2, :], ALU.mult,
                       [Rb[ob], Rt[rc]], Rd)
                    tt(S.dve, dst[64:128], ov[64:128, 1:4:2, :], rv[0:64, 1:4:2, :], ALU.mult,
                       [Rb[ob], Rt[rc]], Rd)
                    TMP.free(rc)
                    ACCA.free(ob)
            drain(99)

            for mg in range(2):
                kpb = wload(HT["WpB"][mg])
                wpb = wslot[kpb][:].rearrange("p (k c) -> p k c", k=4)
                for ml in range(4):
                    m = mg * 4 + ml
                    pbk = ACC.get()
                    for kc in range(4):
                        mm(bank(pbk), wpb[:, kc, ml * 128:(ml + 1) * 128], arch(12 + kc),
                           kc == 0, kc == 3, [Rw[kpb], Rar[12 + kc]], [Rb[pbk]])
                    eg = TMP.get()
                    actf(tmpf[eg][:], arch(GBS[m]), AF.Exp, [Rar[GBS[m]]], [Rt[eg]], scale=-1.0)
                    actf(tmpf[eg][:], tmpf[eg][:], AF.Ln, [Rt[eg]], [Rt[eg]], bias=1.0)
                    actf(tmpf[eg][:], tmpf[eg][:], AF.Exp, [Rt[eg]], [Rt[eg]], scale=-1.0)
                    tt(S.dve, tmpf[eg][:], bank(pbk), tmpf[eg][:], ALU.mult, [Rb[pbk], Rt[eg]], [Rt[eg]])
                    ACC.free(pbk)
                    tt(S.dve, arch(16 + m), arch(16 + m), tmpf[eg][:], ALU.add, [Rar[16 + m], Rt[eg]], [Rar[16 + m]])
                    TMP.free(eg)
                WS.free(kpb)

            kws = [wload(t) for t in HT["Wout"]]
            w3s = [wslot[k][:].rearrange("p (k c) -> p k c", k=4) for k in kws]
            hn_js = []
            for s in range(4):
                bs = [ACC.get(), ACC.get()]
                for hf in range(2):
                    for kq in range(2):
                        k = kws[hf * 2 + kq]
                        for kc in range(4):
                            mm(bank(bs[hf]), arch(16 + kq * 4 + kc)[:, s * 128:(s + 1) * 128],
                               w3s[hf * 2 + kq][:, kc, :], kq == 0 and kc == 0, kq == 1 and kc == 3,
                               [Rw[k], Rar[16 + kq * 4 + kc]], [Rb[bs[hf]]])
                for hf in range(2):
                    hs = hbuf[:, s, hf * 512:(hf + 1) * 512]
                    tt(S.dve, hs, bank(bs[hf]), hs, ALU.add, [Rb[bs[hf]], Rh[s]], [Rh[s]])
                    ACC.free(bs[hf])
                si, rs = rstd_of(hbuf[:, s, :], [Rh[s]])
                j = TMP.get()
                tsmul(S.dve, tb(j)[:, 0:1024], hbuf[:, s, :], rs, [Rh[s], Rst[si]], [Rt[j]])
                STP.free(si)
                hn_js.append(j)
                if s >= 1:
                    transpose_to(tb(hn_js[s - 1]), [Rt[hn_js[s - 1]]], bufi, s - 1)
                    TMP.free(hn_js[s - 1])
            for k in kws:
                WS.free(k)
            transpose_to(tb(hn_js[3]), [Rt[hn_js[3]]], bufi, 3)
            TMP.free(hn_js[3])

            for tix, t in enumerate(HT["Wup"]):
                k = wload(t)
                w3 = wslot[k][:].rearrange("p (k c) -> p k c", k=8)
                for fl in range(2):
                    f = tix * 2 + fl
                    pool_ = ACC if f % 2 == 0 else ROT
                    zb = pool_.get()
                    for kc in range(8):
                        mm(bank(zb), w3[:, kc, fl * 128:(fl + 1) * 128], xnT[bufi][:, kc, :], kc == 0, kc == 7,
                           [Rw[k], RxnT[bufi]], [Rb[zb]])
                    r_ = TMP.get()
                    actf(tmpf[r_][:], bank(zb), AF.Relu, [Rb[zb]], [Rt[r_]])
                    pool_.free(zb)
                    tt(S.pool if f % 3 == 0 else S.dve, arch(f), tmpf[r_][:], tmpf[r_][:], ALU.mult,
                       [Rt[r_]], [Rar[f]])
                    TMP.free(r_)
                WS.free(k)

            if i + 2 < nt:
                carry = norm_stage(row_base + (i + 2) * T, on_pool=True)
            if i + 1 < nt:
                build_rope_tiles(i + 1)
                carry_q = wload(HT["qA"][0])
                q0gen = qA_gen(0, ROT, ROT, buf=(i + 1) % 2, out_ap=q0buf[:], Rout=[R("q0")], k=carry_q)
            else:
                q0gen = None

            for hf in range(2):
                bs = [ACC.get() for _ in range(4)]
                for kq in range(8):
                    k = wload(HT["Wdown"][hf * 8 + kq])
                    w3 = wslot[k][:].rearrange("p (k c) -> p k c", k=4)
                    for s in range(4):
                        for kc in range(4):
                            f = kq * 4 + kc
                            mm(bank(bs[s]), arch(f)[:, s * 128:(s + 1) * 128], w3[:, kc, :],
                               kq == 0 and kc == 0, kq == 7 and kc == 3, [Rw[k], Rar[f]], [Rb[bs[s]]])
                    WS.free(k)
                    if q0gen is not None and (hf * 8 + kq) >= 2 and (hf * 8 + kq) % 2 == 0:
                        try:
                            next(q0gen)
                        except StopIteration:
                            q0gen = None
                for s in range(4):
                    hs = hbuf[:, s, hf * 512:(hf + 1) * 512]
                    tt(S.dve, hs, bank(bs[s]), hs, ALU.add, [Rb[bs[s]], Rh[s]], [Rh[s]])
                    ACC.free(bs[s])

            if q0gen is not None:
                for _ in q0gen:
                    pass
                q0gen = None

            si, rss = rstd_batch([(hbuf[:, s, :], [Rh[s]]) for s in range(4)])
            for s in range(4):
                stt(S.dve, hbuf[:, s, :], hbuf[:, s, :], rss[s], gfb[:], ALU.mult, ALU.mult,
                    [Rh[s], Rst[si], R("gfb")], [Rh[s]])
                r0s = r0 + s * 128
                pdma(c_out[s], y_all[r0s:r0s + 128, :], hbuf[:, s, :], [Rh[s]], [])
            STP.free(si)

    row = 0
    for _ in cast_gen():
        pass
    for SL in seq_lens:
        process_sequence(row, SL, None)
        row += SL

    S.final_wait(S.sp, c_out)
    S.finalize_and_emit(st)
    st.close()
    return nc


def _stat8(W):
    return W.reshape(8, 128, 256).transpose(1, 0, 2).reshape(128, 2048)


def _k4(W):
    return W.reshape(4, 128, 512).transpose(1, 0, 2).reshape(128, 2048)


def _pack_weights(w_in, w_proj_a, w_proj_b, w_out, w_up, w_down):
    wt = np.empty((NH, 128, 2048), np.float32)
    perm = np.concatenate([np.r_[j * 64:(j + 1) * 64, (4 + j) * 64:(5 + j) * 64] for j in range(4)])
    qA = w_in[:, 0:512][:, perm]
    for h, t in enumerate(HT["qA"]):
        wt[t] = _stat8(qA[:, h * 256:(h + 1) * 256])
    wt[HT["kvA"][0]] = _stat8(w_in[:, 512:768])
    for h, t in enumerate(HT["qB"]):
        wt[t] = _stat8(w_in[:, 768 + h * 256:768 + (h + 1) * 256])
    for h, t in enumerate(HT["kB"]):
        wt[t] = _stat8(w_in[:, 1280 + h * 256:1280 + (h + 1) * 256])
    vB = w_in[:, 1792:2304]
    for q, t in enumerate(HT["vB"]):
        wt[t] = _k4(vB[q * 512:(q + 1) * 512, :])
    for h, t in enumerate(HT["gA"]):
        wt[t] = _stat8(w_in[:, 2304 + h * 256:2304 + (h + 1) * 256])
    for h, t in enumerate(HT["gB"]):
        wt[t] = _stat8(w_in[:, 3328 + h * 256:3328 + (h + 1) * 256])
    wpa = w_proj_a[perm, :]
    for h, t in enumerate(HT["WpA"]):
        wt[t] = _k4(wpa[:, h * 512:(h + 1) * 512])
    for h, t in enumerate(HT["WpB"]):
        wt[t] = _k4(w_proj_b[:, h * 512:(h + 1) * 512])
    for hf in range(2):
        for kq in range(2):
            wt[HT["Wout"][hf * 2 + kq]] = _k4(w_out[kq * 512:(kq + 1) * 512, hf * 512:(hf + 1) * 512])
    for h, t in enumerate(HT["Wup"]):
        wt[t] = _stat8(w_up[:, h * 256:(h + 1) * 256])
    for hf in range(2):
        for kq in range(8):
            wt[HT["Wdown"][hf * 8 + kq]] = _k4(w_down[kq * 512:(kq + 1) * 512, hf * 512:(hf + 1) * 512])
    return wt


def _constants():
    bf = ml_dtypes.bfloat16
    ident = np.eye(128, dtype=np.float32)
    bones = np.zeros((128, 128), np.float32)
    bones[0:64, 0:64] = 1.0 / 64
    bones[64:128, 64:128] = 1.0 / 64
    rotm = np.zeros((128, 128), np.float32)
    for i in range(64):
        rotm[2 * i + 1, 2 * i] = -1.0
        rotm[2 * i, 2 * i + 1] = 1.0
    cbf = np.concatenate([ident, bones, rotm], axis=1).astype(bf)
    inv = 10000.0 ** (-np.arange(0, 32, 2, dtype=np.float64) / 32.0)
    tabs = np.zeros((128, 4, 64), np.float64)
    pos = np.arange(64, dtype=np.float64)
    for p in range(128):
        i = (p % 64) // 2
        if i < 16:
            ang = pos * np.float64(np.float32(inv[i]))
            tabs[p, 0, :] = np.cos(ang)
            tabs[p, 1, :] = np.sin(ang)
        else:
            ang = pos * np.float64(np.float32(inv[i - 16]))
            tabs[p, 2, :] = np.cos(ang)
            tabs[p, 3, :] = np.sin(ang)
    tabs = tabs.astype(np.float32).reshape(128, 256)
    ind = np.zeros((32, 4096), np.float32)
    kc = np.arange(64)[:, None]
    qc = np.arange(64)[None, :]
    b = kc - qc + 15
    for bb in range(31):
        ind[bb] = (b == bb).astype(np.float32).reshape(-1)
    cstart = np.clip(qc - 8, 0, 48)
    cm = (kc >= cstart) & (kc < cstart + 16)
    ind[31] = np.where(cm, 0.0, -30000.0).astype(np.float32).reshape(-1)
    return cbf, tabs, ind


SEQ_LENS = [4096, 2048, 2048, 2048, 2048]
_CACHE = {}


def _shared_inputs(norm_mix_g, w_in, a_q_norm_g, a_k_norm_g, b_rel_pos_bias, w_proj_a, w_proj_b,
                   w_out, norm_mlp_g, w_mlp_up, w_mlp_down, norm_final_g):
    f = lambda a: np.ascontiguousarray(np.asarray(a, dtype=np.float32))
    wt = _pack_weights(f(w_in)[0], f(w_proj_a)[0], f(w_proj_b)[0], f(w_out)[0], f(w_mlp_up)[0],
                       f(w_mlp_down)[0])
    g1 = f(norm_mix_g)[0].reshape(8, 128).T
    g2 = f(norm_mlp_g)[0].reshape(8, 128).T
    gtabs = np.ascontiguousarray(np.concatenate([g1, g2], axis=1))
    smalls = np.ascontiguousarray(np.stack([np.tile(f(a_q_norm_g)[0], 2), np.tile(f(a_k_norm_g)[0], 2)], axis=1))
    gfb = np.ascontiguousarray(np.broadcast_to(f(norm_final_g)[None, :], (128, D)))
    rpb = f(b_rel_pos_bias)[0]
    rpbT = np.ones((32, 120), np.float32)
    rpbT[0:31] = rpb.reshape(120, 31).T
    cbf, tabs, ind = _constants()
    return {"wt": wt, "gtabs": gtabs, "smalls": smalls, "gfb": gfb, "rpbT": rpbT, "ind": ind,
            "cbf": cbf, "tabs": tabs}


def kernel(x_prompt, x_sample, norm_mix_g, w_in, a_q_norm_g, a_k_norm_g, b_rel_pos_bias,
           w_proj_a, w_proj_b, w_out, norm_mlp_g, w_mlp_up, w_mlp_down, norm_final_g):
    n = 8
    x_prompt = np.asarray(x_prompt, dtype=np.float32)
    x_sample = np.asarray(x_sample, dtype=np.float32)
    shared = _shared_inputs(norm_mix_g, w_in, a_q_norm_g, a_k_norm_g, b_rel_pos_bias, w_proj_a,
                            w_proj_b, w_out, norm_mlp_g, w_mlp_up, w_mlp_down, norm_final_g)
    nc = bass.Bass("TRN2", target_bir_lowering=False)
    build_program(nc, SEQ_LENS)
    in_maps = []
    for c in range(n):
        xa = np.concatenate([x_prompt[c].reshape(4096, D), x_sample[4 * c:4 * c + 4].reshape(8192, D)], axis=0)
        m = dict(shared)
        m["x_all"] = np.ascontiguousarray(xa)
        in_maps.append(m)
    res = run_bass_kernel_spmd(nc, in_maps, core_ids=list(range(n)))
    yp = np.empty((8, 4096, D), np.float32)
    ys = np.empty((32, 2048, D), np.float32)
    for c in range(n):
        y = res.results[c]["y_all"]
        yp[c] = y[0:4096]
        ys[4 * c:4 * c + 4] = y[4096:].reshape(4, 2048, D)
    return (yp, ys)
```
